# Optimizing a Trainium2 kernel written in Bass

```python
import jax, jax.numpy as jnp
from jax import lax
import numpy as np

D_MODEL = 2048
BATCH = 4
SEQ = 8192
DEPTH = 1

GRID_W = 64
CTX_LEN = 256
N_Q_HEADS = 16
N_KV_HEADS = 4
HEAD_DIM = 128
ATTN_Q_W = N_Q_HEADS * HEAD_DIM
ATTN_KV_W = N_KV_HEADS * HEAD_DIM
ROPE_THETA = 10000.0
Q_BLOCK = 128
RWKV_HEADS = 16
RWKV_HEAD = 64
RWKV_W = RWKV_HEADS * RWKV_HEAD
W_LORA = 64
A_LORA = 64
G_LORA = 128
RWKV_SPLITS = (RWKV_W, RWKV_W, RWKV_W, W_LORA, W_LORA, A_LORA, A_LORA, G_LORA)
RWKV_IN_W = 3 * RWKV_W + 2 * W_LORA + 2 * A_LORA + G_LORA
IN_SPLITS = (ATTN_Q_W, ATTN_KV_W, ATTN_KV_W, RWKV_IN_W, D_MODEL, D_MODEL)
IN_W = ATTN_Q_W + 2 * ATTN_KV_W + RWKV_IN_W + 2 * D_MODEL
D_FF = 5632
N_MOD = 9
EPS = 1e-6
GN_EPS = 64e-5

kernel_name = "hybrid_gqa_rwkv7_macaron_dit"


def split_cols(t, sizes):
    return jnp.split(t, np.cumsum(sizes)[:-1].tolist(), axis=-1)


def rms_norm(x, g):
    xf = x.astype(jnp.float32)
    y = xf * lax.rsqrt(jnp.mean(xf * xf, axis=-1, keepdims=True) + EPS)
    return y.astype(x.dtype) * g


def modulate(h, shift, scale):
    return h * (1 + scale) + shift


def swiglu(h, w_in, w_out):
    gt, up = jnp.split(h @ w_in, 2, axis=-1)
    return (jax.nn.silu(gt) * up) @ w_out


def ffn_sublayer(x, shift, scale, gate, g, w_in, w_out):
    return x + 0.5 * gate * swiglu(modulate(rms_norm(x, g), shift, scale), w_in, w_out)


def head_rms(t, g):
    tf = t.astype(jnp.float32)
    return (tf * lax.rsqrt(jnp.mean(tf * tf, axis=-1, keepdims=True) + EPS)).astype(t.dtype) * g


def grid_rope_tables(n_tok, dtype):
    rows = n_tok // GRID_W
    row = jnp.repeat(jnp.arange(rows, dtype=jnp.int32), GRID_W)
    col = jnp.tile(jnp.arange(GRID_W, dtype=jnp.int32), rows)
    axis_dim = HEAD_DIM // 2
    inv = ROPE_THETA ** (-jnp.arange(0, axis_dim, 2, dtype=jnp.float32) / axis_dim)
    ang_r = row.astype(jnp.float32)[:, None] * inv[None]
    ang_c = col.astype(jnp.float32)[:, None] * inv[None]
    tab = lambda a: (jnp.cos(a)[None, :, None, :].astype(dtype), jnp.sin(a)[None, :, None, :].astype(dtype))
    return tab(ang_r), tab(ang_c)


def rope_1d(t, cos, sin):
    t1, t2 = jnp.split(t, 2, axis=-1)
    return jnp.concatenate([t1 * cos - t2 * sin, t2 * cos + t1 * sin], axis=-1)


def axial_rope(t, tables):
    (cr, sr), (cc, sc) = tables
    t_row, t_col = jnp.split(t, 2, axis=-1)
    return jnp.concatenate([rope_1d(t_row, cr, sr), rope_1d(t_col, cc, sc)], axis=-1)


def gqa_blocks(q, k, v):
    B, T, _, hd = q.shape
    G = N_Q_HEADS // N_KV_HEADS
    nb = T // Q_BLOCK
    qb = jnp.moveaxis(q.reshape(B, nb, Q_BLOCK, N_KV_HEADS, G, hd), 1, 0)
    scale = hd ** -0.5

    def one(qblk):
        s = jnp.einsum('bqhgd,bkhd->bhgqk', qblk, k, preferred_element_type=jnp.float32) * scale
        p = jax.nn.softmax(s, axis=-1)
        return jnp.einsum('bhgqk,bkhd->bqhgd', p.astype(v.dtype), v)

    o = lax.map(one, qb)
    return jnp.moveaxis(o, 0, 1).reshape(B, T, N_Q_HEADS * hd)


def centered_shift(p):
    prev = jnp.pad(p[:, :-1], ((0, 0), (1, 0), (0, 0)))
    nxt = jnp.pad(p[:, 1:], ((0, 0), (0, 1), (0, 0)))
    return 0.5 * (prev + nxt)


def rwkv_prepare(p, mu, w0, w2, a0, a2, g2, k_k, k_a):
    B, T, _ = p.shape
    p = p + mu * (centered_shift(p) - p)
    r, k, v, lw_f, lw_b, la_f, la_b, lg = split_cols(p, RWKV_SPLITS)
    lw = jnp.stack([lw_f, lw_b])
    la = jnp.stack([la_f, la_b])
    w_raw = (w0[:, None, None, :] + jnp.einsum('zbtr,zrc->zbtc', jnp.tanh(lw), w2)).astype(jnp.float32)
    decay = jnp.exp(-jnp.exp(-jax.nn.softplus(-w_raw) - 0.5))
    a = jax.nn.sigmoid((a0[:, None, None, :] + jnp.einsum('zbtr,zrc->zbtc', la, a2)).astype(jnp.float32))
    g = jax.nn.sigmoid(lg) @ g2
    hs = lambda t: t.reshape(t.shape[:-1] + (RWKV_HEADS, RWKV_HEAD))
    rf, kf, vf = hs(r.astype(jnp.float32)), hs(k.astype(jnp.float32)), hs(v.astype(jnp.float32))
    kk = kf * hs(k_k.astype(jnp.float32))
    kk = kk / jnp.maximum(jnp.sqrt(jnp.sum(kk * kk, axis=-1, keepdims=True)), 1e-12)
    a_h = hs(a)
    k_mod = kf[None] * (1 + (a_h - 1) * hs(k_a.astype(jnp.float32)))
    b = kk[None] * a_h
    return rf, hs(decay), k_mod, vf, kk, b, g


def rwkv_scan(S0, feats, emit):
    rf, decay, k_mod, vf, kk, b, _ = feats
    shared = lambda t: jnp.moveaxis(jnp.stack([t, t[:, ::-1]]), 2, 0)
    perdir = lambda t: jnp.moveaxis(jnp.stack([t[0], t[1][:, ::-1]]), 2, 0)
    xs = (shared(rf), perdir(decay), perdir(k_mod), shared(vf), shared(-kk), perdir(b))

    def step(S, inp):
        r_t, w_t, k_t, v_t, a_t, b_t = inp
        sa = jnp.einsum('zbhvk,zbhk->zbhv', S, a_t)
        S = S * w_t[..., None, :] + sa[..., None] * b_t[..., None, :] + v_t[..., None] * k_t[..., None, :]
        y = jnp.einsum('zbhvk,zbhk->zbhv', S, r_t) if emit else None
        return S, y

    S, ys = lax.scan(step, S0, xs)
    if not emit:
        return S, None
    ys = jnp.moveaxis(ys, 0, 2)
    return S, ys[0] + ys[1][:, ::-1]


def rwkv_output(y, feats, r_k, ln_w, ln_b, dtype):
    rf, _, k_mod, vf, _, _, g = feats
    B, T, H, N = y.shape
    mu = jnp.mean(y, axis=-1, keepdims=True)
    var = jnp.mean(jnp.square(y - mu), axis=-1, keepdims=True)
    yn = ((y - mu) * lax.rsqrt(var + GN_EPS)).reshape(B, T, H * N) * ln_w + ln_b
    bonus = jnp.sum(rf[None] * k_mod * r_k.astype(jnp.float32), axis=(0, -1))[..., None] * vf
    return ((yn + bonus.reshape(B, T, H * N)) * g).astype(dtype)


def merge(ga, gr, attn_o, rwkv_o, w_oa, w_or, w_out):
    return (jax.nn.sigmoid(ga) * (attn_o @ w_oa) + jax.nn.sigmoid(gr) * (rwkv_o @ w_or)) @ w_out


def setup_inputs(seed: int = 0) -> dict:
    key = jax.random.key(seed)
    ks = jax.random.split(key, 36)
    f32 = jnp.float32
    L, D = DEPTH, D_MODEL
    nrm = lambda k, shape, s: jax.random.normal(k, shape, f32) * s
    return {
        "x": nrm(ks[0], (BATCH, SEQ, D), 1.0),
        "c": nrm(ks[1], (BATCH, D), 1.0),
        "ctx": nrm(ks[2], (BATCH, CTX_LEN, D), 1.0),
        "c_ctx": nrm(ks[3], (D,), 1.0),
        "w_mod": nrm(ks[4], (L, D, N_MOD * D), D ** -0.5),
        "b_mod": nrm(ks[5], (L, N_MOD * D), 0.01),
        "norm_ffn1": 1.0 + nrm(ks[6], (L, D), 0.01),
        "ffn1_w_in": nrm(ks[7], (L, D, 2 * D_FF), D ** -0.5),
        "ffn1_w_out": nrm(ks[8], (L, D_FF, D), D_FF ** -0.5),
        "norm_mix": 1.0 + nrm(ks[9], (L, D), 0.01),
        "w_in": nrm(ks[10], (L, D, IN_W), D ** -0.5),
        "q_norm": 1.0 + nrm(ks[11], (L, HEAD_DIM), 0.01),
        "k_norm": 1.0 + nrm(ks[12], (L, HEAD_DIM), 0.01),
        "rwkv_mu": jax.random.uniform(ks[13], (L, RWKV_IN_W), f32),
        "w0": jax.random.uniform(ks[14], (L, 2, RWKV_W), f32, -6.0, -1.0),
        "w2": nrm(ks[15], (L, 2, W_LORA, RWKV_W), 0.5 * W_LORA ** -0.5),
        "a0": nrm(ks[16], (L, 2, RWKV_W), 0.1),
        "a2": nrm(ks[17], (L, 2, A_LORA, RWKV_W), A_LORA ** -0.5),
        "g2": nrm(ks[18], (L, G_LORA, RWKV_W), G_LORA ** -0.5),
        "k_k": 0.85 + nrm(ks[19], (L, RWKV_W), 0.02),
        "k_a": 1.0 + nrm(ks[20], (L, RWKV_W), 0.02),
        "r_k": nrm(ks[21], (L, RWKV_HEADS, RWKV_HEAD), 0.1),
        "ln_x_w": 1.0 + nrm(ks[22], (L, RWKV_W), 0.01),
        "ln_x_b": nrm(ks[23], (L, RWKV_W), 0.01),
        "w_oa": nrm(ks[24], (L, ATTN_Q_W, D), ATTN_Q_W ** -0.5),
        "w_or": nrm(ks[25], (L, RWKV_W, D), RWKV_W ** -0.5),
        "w_out": nrm(ks[26], (L, D, D), D ** -0.5),
        "norm_ffn2": 1.0 + nrm(ks[27], (L, D), 0.01),
        "ffn2_w_in": nrm(ks[28], (L, D, 2 * D_FF), D ** -0.5),
        "ffn2_w_out": nrm(ks[29], (L, D_FF, D), D_FF ** -0.5),
        "norm_final": 1.0 + nrm(ks[30], (D,), 0.01),
    }


def reference(x, c, ctx, c_ctx, w_mod, b_mod, norm_ffn1, ffn1_w_in, ffn1_w_out, norm_mix, w_in,
              q_norm, k_norm, rwkv_mu, w0, w2, a0, a2, g2, k_k, k_a, r_k, ln_x_w, ln_x_b,
              w_oa, w_or, w_out, norm_ffn2, ffn2_w_in, ffn2_w_out, norm_final):
    B, S, D = x.shape
    tables = grid_rope_tables(S, x.dtype)
    cx = ctx
    hs = lambda t, h: t.reshape(t.shape[:-1] + (h, HEAD_DIM))
    for i in range(DEPTH):
        is_last = i == DEPTH - 1
        mod = (jax.nn.silu(c) @ w_mod[i] + b_mod[i]).reshape(B, N_MOD, 1, D)
        mod_c = (jax.nn.silu(c_ctx) @ w_mod[i] + b_mod[i]).reshape(N_MOD, D)

        x = ffn_sublayer(x, mod[:, 0], mod[:, 1], mod[:, 2], norm_ffn1[i], ffn1_w_in[i], ffn1_w_out[i])
        cx = ffn_sublayer(cx, mod_c[0], mod_c[1], mod_c[2], norm_ffn1[i], ffn1_w_in[i], ffn1_w_out[i])

        h = modulate(rms_norm(x, norm_mix[i]), mod[:, 3], mod[:, 4])
        hc = modulate(rms_norm(cx, norm_mix[i]), mod_c[3], mod_c[4])
        q, k, v, p_rw, ga, gr = split_cols(h @ w_in[i], IN_SPLITS)
        qc, kc, vc, p_rw_c, ga_c, gr_c = split_cols(hc @ w_in[i], IN_SPLITS)

        q = axial_rope(head_rms(hs(q, N_Q_HEADS), q_norm[i]), tables)
        k = axial_rope(head_rms(hs(k, N_KV_HEADS), k_norm[i]), tables)
        qc = head_rms(hs(qc, N_Q_HEADS), q_norm[i])
        kc = head_rms(hs(kc, N_KV_HEADS), k_norm[i])
        v, vc = hs(v, N_KV_HEADS), hs(vc, N_KV_HEADS)
        attn_lat = gqa_blocks(q, jnp.concatenate([k, kc], axis=1), jnp.concatenate([v, vc], axis=1))

        rw_args = (rwkv_mu[i], w0[i], w2[i], a0[i], a2[i], g2[i], k_k[i], k_a[i])
        feats_c = rwkv_prepare(p_rw_c, *rw_args)
        S0 = jnp.zeros((2, B, RWKV_HEADS, RWKV_HEAD, RWKV_HEAD), jnp.float32)
        S_ctx, y_c = rwkv_scan(S0, feats_c, emit=not is_last)
        feats = rwkv_prepare(p_rw, *rw_args)
        _, y = rwkv_scan(S_ctx, feats, emit=True)
        rwkv_lat = rwkv_output(y, feats, r_k[i], ln_x_w[i], ln_x_b[i], x.dtype)

        x = x + mod[:, 5] * merge(ga, gr, attn_lat, rwkv_lat, w_oa[i], w_or[i], w_out[i])

        if not is_last:
            attn_c = gqa_blocks(qc, kc, vc)
            rwkv_c = rwkv_output(y_c, feats_c, r_k[i], ln_x_w[i], ln_x_b[i], cx.dtype)
            cx = cx + mod_c[5] * merge(ga_c, gr_c, attn_c, rwkv_c, w_oa[i], w_or[i], w_out[i])
            cx = ffn_sublayer(cx, mod_c[6], mod_c[7], mod_c[8], norm_ffn2[i], ffn2_w_in[i], ffn2_w_out[i])

        x = ffn_sublayer(x, mod[:, 6], mod[:, 7], mod[:, 8], norm_ffn2[i], ffn2_w_in[i], ffn2_w_out[i])

    return rms_norm(x, norm_final)
```

```python
import contextlib
import numpy as np
import concourse.bass as bass
import concourse.mybir as mybir
from concourse.bass_utils import run_bass_kernel_spmd

F32 = mybir.dt.float32
BF16 = mybir.dt.bfloat16
AF = mybir.ActivationFunctionType
ALU = mybir.AluOpType

D = 2048
NCH = 16
DFF = 5632
NJ = 44
NMOD = 9
EPS = 1e-6


class Res:
    __slots__ = ("w", "r", "name")

    def __init__(self, name):
        self.name = name
        self.w = None
        self.r = {}


class KB:
    def __init__(self, nc, es):
        self.nc = nc
        self.es = es
        self.engs = {"pe": nc.tensor, "act": nc.scalar, "dve": nc.vector, "pool": nc.gpsimd, "sp": nc.sync}
        self.sems = {}
        self.cnt = {}
        for e in self.engs:
            self.sems[e] = es.enter_context(nc.semaphore("c_" + e))
            self.cnt[e] = 0
        self.waited = {e: {} for e in self.engs}
        self.res = {}
        self.n_inst = 0
        self.n_wait = 0

    def sb(self, name, shape, dt):
        return self.es.enter_context(self.nc.sbuf_tensor("sb_" + name, list(shape), dt))

    def ps(self, name, shape, dt=F32):
        return self.es.enter_context(self.nc.psum_tensor("ps_" + name, list(shape), dt))

    def R(self, key):
        r = self.res.get(key)
        if r is None:
            r = self.res[key] = Res(key)
        return r

    def lane(self, name):
        if name not in self.sems:
            self.sems[name] = self.es.enter_context(self.nc.semaphore("l_" + name))
            self.cnt[name] = 0
        return name

    def _wait(self, eng, tok, same_engine_ok=False):
        if tok is None:
            return
        key, val = tok
        if key == eng and same_engine_ok:
            return
        if self.waited[eng].get(key, 0) >= val:
            return
        self.engs[eng].wait_ge(self.sems[key], val)
        self.waited[eng][key] = val
        self.n_wait += 1

    def _deps(self, eng, reads, writes, pe_accum=False):
        for k in reads:
            r = self.R(k)
            self._wait(eng, r.w)
        for k in writes:
            r = self.R(k)
            self._wait(eng, r.w, same_engine_ok=True)
            for sk, v in r.r.items():
                self._wait(eng, (sk, v), same_engine_ok=True)

    def _commit(self, tok, reads, writes):
        for k in reads:
            r = self.R(k)
            if r.r.get(tok[0], 0) < tok[1]:
                r.r[tok[0]] = tok[1]
        for k in writes:
            r = self.R(k)
            r.w = tok
            r.r = {}

    def op(self, eng, fn, reads=(), writes=()):
        self._deps(eng, reads, writes)
        inst = fn()
        self.cnt[eng] += 1
        inst.then_inc(self.sems[eng], 1)
        tok = (eng, self.cnt[eng])
        self._commit(tok, reads, writes)
        self.n_inst += 1
        return tok

    def dma(self, q, lane, out, in_, reads=(), writes=(), **kw):
        self.lane(lane)
        self._deps(q, reads, writes)
        prev = self.cnt[lane]
        if prev:
            self._wait(q, (lane, prev))
        inst = self.engs[q].dma_start(out=out, in_=in_, **kw)
        self.cnt[lane] += 16
        inst.then_inc(self.sems[lane], 16)
        tok = (lane, self.cnt[lane])
        self._commit(tok, reads, writes)
        self.n_inst += 1
        return tok

    def wait_all(self, eng, keys):
        for k in keys:
            r = self.R(k)
            self._wait(eng, r.w)
            for sk, v in r.r.items():
                self._wait(eng, (sk, v))


class WStream:
    def __init__(self, kb, plan, nslots=8, kcmax=16):
        self.kb = kb
        self.plan = plan
        self.ns = nslots
        self.slots = [kb.sb(f"wslot{i}", [128, kcmax, 128], BF16) for i in range(nslots)]
        self.issued = 0
        self.used = 0
        for _ in range(min(nslots, len(plan))):
            self._issue()

    def _issue(self):
        i = self.issued
        ap, kc = self.plan[i]
        s = i % self.ns
        self.kb.dma("pool", f"w{s}", self.slots[s][:, 0:kc, :], ap, writes=[("wslot", s)])
        self.issued += 1

    def next(self, expect=None):
        i = self.used
        ap, kc = self.plan[i]
        if expect is not None:
            assert expect is ap or expect == ap, (i, expect, ap)
        self.used += 1
        s = i % self.ns
        return self.slots[s], ("wslot", s), i

    def done(self, i):
        if self.issued < len(self.plan) and self.issued == i + self.ns:
            self._issue()


def fm_vec(v):
    v = np.asarray(v, np.float32)
    return np.ascontiguousarray(v.reshape(-1, 128).T)


def prep_w(W):
    K, N = W.shape
    return np.ascontiguousarray(W.reshape(K // 128, 128, N // 128, 128).transpose(2, 1, 0, 3))


V_C = 0
V_BMOD = 32
V_G1 = 176
V_GMIX = 192
V_G2 = 208
V_GF = 224
V_QN = 240
V_KN = 241
NV = 242
CTX = 256
GN_EPS = 64e-5
CB = 128
DECAY_C = -0.6065306597126334
HV_MU = 0
HV_KK = 48
HV_KA = 64
HV_A0 = 80
HV_RK = 112
HV_LW = 128
HV_LB = 144
NHV = 160
NQH = 16
NKV = 4
ATT_SCALE = 128 ** -0.5


def rope_tables(T):
    rows = T // 64
    row = np.repeat(np.arange(rows, dtype=np.float32), 64)
    col = np.tile(np.arange(64, dtype=np.float32), rows)
    inv = (np.float32(10000.0) ** (-np.arange(0, 64, 2, dtype=np.float32) / np.float32(64))).astype(np.float32)
    ar = (row[:, None] * inv[None]).astype(np.float32)
    ac = (col[:, None] * inv[None]).astype(np.float32)
    cos = np.concatenate([np.cos(ar), np.cos(ar), np.cos(ac), np.cos(ac)], 1).T
    sin = np.concatenate([-np.sin(ar), np.sin(ar), -np.sin(ac), np.sin(ac)], 1).T
    perm = np.zeros((128, 128), np.float32)
    for d in range(128):
        partner = d + 32 if (d % 64) < 32 else d - 32
        perm[partner, d] = 1.0
    return np.ascontiguousarray(cos, np.float32), np.ascontiguousarray(sin, np.float32), perm


def rwkv_consts():
    ii = np.arange(CB)
    tri = np.zeros((4, CB, CB), np.float32)
    tri[0] = (ii[:, None] <= ii[None, :]) * DECAY_C
    tri[1] = (ii[:, None] < ii[None, :]) * DECAY_C
    tri[2] = (ii[:, None] >= ii[None, :]) * DECAY_C
    tri[3] = (ii[:, None] > ii[None, :]) * DECAY_C
    mA = np.zeros((2, CB, 4 * CB), np.float32)
    mB = np.zeros((2, CB, CB), np.float32)
    for z in range(2):
        before = (ii[:, None] < ii[None, :]) if z == 0 else (ii[:, None] > ii[None, :])
        beq = before | np.eye(CB, dtype=bool)
        mA[z] = np.concatenate([before, before.T, before, beq], 1)
        mB[z] = beq
    mq = np.zeros((2, 8, CB, CB), np.float32)
    for lv in range(7):
        blk = ii // (2 ** lv)
        q0 = (blk[:, None] % 2 == 1) & (blk[None, :] == blk[:, None] - 1)
        mq[0, lv] = q0
        mq[1, lv] = q0.T
    mq[0, 7] = mq[0, 0].T
    mq[1, 7] = mq[1, 0].T
    return (np.ascontiguousarray(tri.transpose(1, 0, 2)), np.ascontiguousarray(mA.transpose(1, 0, 2)),
            np.ascontiguousarray(mB.transpose(1, 0, 2)), np.eye(128, dtype=np.float32),
            np.ascontiguousarray(mq.transpose(2, 0, 1, 3)))


def head_fm(v):
    return np.ascontiguousarray(np.asarray(v, np.float32).reshape(16, 64).T)


def build_program(cfg):
    T = cfg["T"]
    TT = cfg.get("TT", 512)
    stages = cfg.get("stages", "full")
    rw_on = stages == "full"
    NT = T // TT
    NK = T + CTX
    NKC = NK // 128
    NP = CTX + T + 4
    COL_CTX = 1
    COL_LAT = CTX + 3
    nc = bass.Bass("TRN2", target_bir_lowering=False)
    dt = lambda name, shape, dtype=F32, kind="ExternalInput": nc.dram_tensor(name, list(shape), dtype, kind=kind).ap()
    xT = dt("xT", [D, T])
    cxT = dt("cxT", [D, CTX])
    vecs = dt("vecs", [128, NV])
    wmod = dt("wmod", [NMOD * NCH, 128, NCH, 128])
    w1i = dt("w1i", [2 * NJ, 128, NCH, 128])
    w1o = dt("w1o", [NCH, 128, NJ, 128])
    outT = dt("outT", [D, T], kind="ExternalOutput")
    w2i = dt("w2i", [2 * NJ, 128, NCH, 128])
    w2o = dt("w2o", [NCH, 128, NJ, 128])
    wq = dt("wq", [16, 128, NCH, 128])
    wk = dt("wk", [4, 128, NCH, 128])
    wv = dt("wv", [4, 128, NCH, 128])
    wga = dt("wga", [16, 128, NCH, 128])
    woa = dt("woa", [16, 128, NCH, 128])
    wout = dt("wout", [16, 128, NCH, 128])
    cosT = dt("cosT", [128, T])
    sinT = dt("sinT", [128, T])
    perm = dt("perm", [128, 128])
    x1T = dt("x1T", [D, T], F32, "Internal")
    qS = dt("qS", [NQH, 128, T], BF16, "Internal")
    kS = dt("kS", [NKV, 128, NK], BF16, "Internal")
    vS = dt("vS", [NKV, 128, NKC, 128], BF16, "Internal")
    gaS = dt("gaS", [NCH, 128, T], BF16, "Internal")
    if rw_on:
        wrw = dt("wrw", [27, 128, NCH, 128])
        wgr = dt("wgr", [16, 128, NCH, 128])
        wor = dt("wor", [16, 128, 8, 128])
        w2p_d = dt("w2p", [128, 2, 1024])
        a2p_d = dt("a2p", [128, 2, 1024])
        g2_d = dt("g2", [128, 1024])
        w0r_d = dt("w0r", [1, 2, 1024])
        hvec_d = dt("hvec", [64, NHV])
        svec_d = dt("svec", [128, 3])
        tri_d = dt("tri", [128, 4, 128])
        mkA_d = dt("mkA", [128, 2, 512])
        mkB_d = dt("mkB", [128, 2, 128])
        id_d = dt("ident", [128, 128])
        mq_d = dt("mq", [128, 2, 8, 128])
        grS = dt("grS", [NCH, 128, T], BF16, "Internal")
        dk_ = "ExternalOutput" if cfg.get("debug") else "Internal"
        prS = dt("prS", [27, 128, NP], F32, dk_)
        yS = dt("yS", [16, 64, T], F32, dk_)
        bS = dt("bS", [16, 64, T], F32, dk_)
        rwS = dt("rwS", [8, 128, T], BF16, dk_)
        if cfg.get("debug"):
            dbgT = dt("dbgT", [2, 12, 128, 512], F32, "ExternalOutput")

    xT_v = xT.rearrange("(c p) t -> p c t", p=128)
    cxT_v = cxT.rearrange("(c p) t -> p c t", p=128)
    outT_v = outT.rearrange("(c p) t -> p c t", p=128)

    plan = []
    for q in range(NMOD * NCH):
        plan.append((wmod[q], NCH))

    def ffn_plan(wi, wo):
        for j in range(NJ):
            plan.append((wi[j], NCH))
            plan.append((wi[NJ + j], NCH))
        for oc in range(NCH):
            for (k0, k1) in ((0, 16), (16, 32), (32, 44)):
                plan.append((wo[oc][:, k0:k1, :], k1 - k0))

    def lin_plan(w, n, kc=NCH):
        for oc in range(n):
            plan.append((w[oc], kc))

    tiles1 = [("ctx", 0, CTX)] + [("lat", it * TT, TT) for it in range(NT)]
    for kind, t0, n in tiles1:
        ffn_plan(w1i, w1o)
        if kind == "lat":
            lin_plan(wq, 16)
        lin_plan(wk, 4)
        lin_plan(wv, 4)
        if kind == "lat":
            lin_plan(wga, 16)
        if rw_on:
            lin_plan(wrw, 27)
            if kind == "lat":
                lin_plan(wgr, 16)
    for it in range(NT):
        lin_plan(woa, 16)
        if rw_on:
            lin_plan(wor, 16, 8)
        lin_plan(wout, 16)
        ffn_plan(w2i, w2o)

    with contextlib.ExitStack() as es:
        kb = KB(nc, es)
        ws = WStream(kb, plan, nslots=6)
        uid = [0]

        def sbs(stack, name, shape, dtype=F32):
            uid[0] += 1
            return stack.enter_context(nc.sbuf_tensor(f"s{uid[0]}_{name}", list(shape), dtype))

        def barrier():
            keys = list(kb.cnt.keys())
            for e in kb.engs:
                for k in keys:
                    if k != e and kb.cnt[k] > 0:
                        kb._wait(e, (k, kb.cnt[k]))

        vt = kb.sb("vecs", [128, NV], F32)
        kb.dma("sp", "misc", vt[:], vecs[:, :], writes=["vecs"])
        ones_d = kb.sb("ones_d", [128, 128], BF16)
        kb.op("dve", lambda: nc.vector.memset(ones_d[:], 1.0 / D), writes=["ones"])
        ones_h = kb.sb("ones_h", [128, 128], BF16)
        kb.op("dve", lambda: nc.vector.memset(ones_h[:], 1.0 / 128), writes=["ones"])
        ones_1 = kb.sb("ones_1", [128, 128], BF16)
        kb.op("dve", lambda: nc.vector.memset(ones_1[:], 1.0), writes=["ones"])
        silu_c = kb.sb("silu_c", [128, NCH * 2], BF16)
        kb.op("act", lambda: nc.scalar.activation(out=silu_c[:], in_=vt[:, V_C:V_C + 32], func=AF.Silu),
              reads=["vecs"], writes=["silu_c"])
        modv = kb.sb("modv", [128, NMOD * NCH, 2], F32)
        banks = [kb.ps(f"bank{i}", [128, 512], F32) for i in range(8)]
        bank_i = [0]
        NROT = [6]

        def bank():
            i = bank_i[0] % NROT[0]
            bank_i[0] += 1
            return banks[i], ("bank", i)

        perm_b = kb.sb("perm_b", [128, 128], BF16)
        kb.dma("pool", "misc2", perm_b[:], perm[:, :], writes=["perm_b"])
        eps_t = kb.sb("eps_t", [128, 1], F32)
        kb.op("dve", lambda: nc.vector.memset(eps_t[:], EPS), writes=["eps_t"])

        for q in range(NMOD * NCH):
            w, wk_, wi_ = ws.next()
            pb, pk = bank()
            for kc in range(NCH):
                kb.op("pe", lambda kc=kc: nc.tensor.matmul(pb[:, 0:2], lhsT=w[:, kc, :], rhs=silu_c[:, 2 * kc:2 * kc + 2],
                                                          start=(kc == 0), stop=(kc == NCH - 1)),
                      reads=[wk_, "silu_c"], writes=[pk])
            ws.done(wi_)
            kb.op("dve", lambda q=q: nc.vector.tensor_scalar(out=modv[:, q, :], in0=pb[:, 0:2],
                                                            scalar1=vt[:, V_BMOD + q:V_BMOD + q + 1], scalar2=None,
                                                            op0=ALU.add),
                  reads=[pk, "vecs"], writes=["modv"])

        mA = kb.sb("mA", [128, 3, NCH, 2], F32)
        mS = kb.sb("mS", [128, 3, NCH, 2], F32)
        mG = kb.sb("mG", [128, 3, NCH, 2], F32)
        for sl, gcol in ((0, V_G1), (1, V_GMIX), (2, V_G2)):
            for wh in range(2):
                sh = modv[:, (3 * sl) * NCH:(3 * sl + 1) * NCH, wh]
                sc = modv[:, (3 * sl + 1) * NCH:(3 * sl + 2) * NCH, wh]
                gt = modv[:, (3 * sl + 2) * NCH:(3 * sl + 3) * NCH, wh]
                kb.op("dve", lambda sl=sl, wh=wh, sc=sc, gcol=gcol: nc.vector.scalar_tensor_tensor(
                    out=mA[:, sl, :, wh], in0=sc, scalar=1.0, in1=vt[:, gcol:gcol + NCH], op0=ALU.add, op1=ALU.mult),
                    reads=["modv", "vecs"], writes=["mA"])
                kb.op("dve", lambda sl=sl, wh=wh, sh=sh: nc.vector.tensor_copy(out=mS[:, sl, :, wh], in_=sh),
                      reads=["modv"], writes=["mS"])
                kb.op("dve", lambda sl=sl, wh=wh, gt=gt: nc.vector.tensor_scalar(
                    out=mG[:, sl, :, wh], in0=gt, scalar1=(0.5 if sl != 1 else 1.0), scalar2=None, op0=ALU.mult),
                    reads=["modv"], writes=["mG"])

        Bf = {}
        tmp_i = [0]

        def alloc_tile_bufs(stack):
            Bf["xt0"] = sbs(stack, "xt0", [128, NCH, TT], F32)
            Bf["sq"] = sbs(stack, "sq", [128, NCH, TT], BF16)
            Bf["hb"] = sbs(stack, "hb", [128, NCH, TT], BF16)
            Bf["tmps"] = [sbs(stack, f"tmp{i}", [128, TT], F32) for i in range(4)]
            Bf["rstd"] = sbs(stack, "rstd", [128, TT], F32)
            Bf["act"] = sbs(stack, "act", [128, NJ, TT], BF16)

        def tmpbuf():
            i = tmp_i[0] % 4
            tmp_i[0] += 1
            return Bf["tmps"][i], ("tmp", i)

        def rms_rstd(src_fn, src_keys, n, ones):
            rstd = Bf["rstd"]
            pb, pk = bank()
            m = len(src_keys)
            for c in range(m):
                kb.op("pe", lambda c=c: nc.tensor.matmul(pb[:, 0:n], lhsT=ones[:], rhs=src_fn(c),
                                                        start=(c == 0), stop=(c == m - 1)),
                      reads=["ones", src_keys[c]], writes=[pk])
            kb.op("act", lambda: nc.scalar.activation(out=rstd[:, 0:n], in_=pb[:, 0:n], func=AF.Sqrt, bias=eps_t[:], scale=1.0),
                  reads=[pk, "eps_t"], writes=["rstd"])
            kb.op("dve", lambda: nc.vector.reciprocal(out=rstd[:, 0:n], in_=rstd[:, 0:n]), reads=["rstd"], writes=["rstd"])

        def rms_modulate(x, xk, sl, wh, n):
            sq, hb, rstd = Bf["sq"], Bf["hb"], Bf["rstd"]
            kb.op("act", lambda: nc.scalar.activation(out=sq[:, :, 0:n], in_=x[:, :, 0:n], func=AF.Square),
                  reads=[xk], writes=["sq"])
            rms_rstd(lambda c: sq[:, c, 0:n], ["sq"] * NCH, n, ones_d)
            for c in range(NCH):
                tb, tk = tmpbuf()
                kb.op("dve", lambda c=c, tb=tb: nc.vector.tensor_tensor(out=tb[:, 0:n], in0=x[:, c, 0:n], in1=rstd[:, 0:n], op=ALU.mult),
                      reads=[xk, "rstd"], writes=[tk])
                kb.op("dve", lambda c=c, tb=tb: nc.vector.tensor_scalar(out=hb[:, c, 0:n], in0=tb[:, 0:n],
                                                                       scalar1=mA[:, sl, c, wh:wh + 1], scalar2=mS[:, sl, c, wh:wh + 1],
                                                                       op0=ALU.mult, op1=ALU.add),
                      reads=[tk, "mA", "mS"], writes=["hb"])

        def linear(src, src_key_fn, n, nchunks, epilogue, kcn=NCH):
            for oc in range(nchunks):
                w, wkey, wi_ = ws.next()
                pb, pk = bank()
                for kc in range(kcn):
                    kb.op("pe", lambda kc=kc: nc.tensor.matmul(pb[:, 0:n], lhsT=w[:, kc, :], rhs=src[:, kc, 0:n],
                                                              start=(kc == 0), stop=(kc == kcn - 1)),
                          reads=[wkey, src_key_fn(kc)], writes=[pk])
                ws.done(wi_)
                epilogue(oc, pb, pk)

        def ffn(x, xk, sl, wh, n):
            hb, act = Bf["hb"], Bf["act"]
            rms_modulate(x, xk, sl, wh, n)
            for j in range(NJ):
                wg, wgk, ig = ws.next()
                wu, wuk, iu = ws.next()
                pg, pgk = bank()
                pu, puk = bank()
                for kc in range(NCH):
                    kb.op("pe", lambda kc=kc: nc.tensor.matmul(pg[:, 0:n], lhsT=wg[:, kc, :], rhs=hb[:, kc, 0:n],
                                                              start=(kc == 0), stop=(kc == NCH - 1)),
                          reads=[wgk, "hb"], writes=[pgk])
                ws.done(ig)
                for kc in range(NCH):
                    kb.op("pe", lambda kc=kc: nc.tensor.matmul(pu[:, 0:n], lhsT=wu[:, kc, :], rhs=hb[:, kc, 0:n],
                                                              start=(kc == 0), stop=(kc == NCH - 1)),
                          reads=[wuk, "hb"], writes=[puk])
                ws.done(iu)
                s, sk = tmpbuf()
                kb.op("act", lambda s=s: nc.scalar.activation(out=s[:, 0:n], in_=pg[:, 0:n], func=AF.Silu),
                      reads=[pgk], writes=[sk])
                kb.op("dve", lambda s=s, j=j: nc.vector.tensor_tensor(out=act[:, j, 0:n], in0=s[:, 0:n], in1=pu[:, 0:n], op=ALU.mult),
                      reads=[sk, puk], writes=[("act", j)])
            for oc in range(NCH):
                po, pok = bank()
                for (k0, k1) in ((0, 16), (16, 32), (32, 44)):
                    w, wkey, wi_ = ws.next()
                    for kc in range(k0, k1):
                        kb.op("pe", lambda kc=kc, k0=k0: nc.tensor.matmul(po[:, 0:n], lhsT=w[:, kc - k0, :], rhs=act[:, kc, 0:n],
                                                                         start=(kc == 0), stop=(kc == NJ - 1)),
                              reads=[wkey, ("act", kc)], writes=[pok])
                    ws.done(wi_)
                kb.op("dve", lambda oc=oc: nc.vector.scalar_tensor_tensor(
                    out=x[:, oc, 0:n], in0=po[:, 0:n], scalar=mG[:, sl, oc, wh:wh + 1], in1=x[:, oc, 0:n],
                    op0=ALU.mult, op1=ALU.add),
                    reads=[pok, "mG", xk], writes=[xk])

        def final_norm_store(x, xk, t0, n):
            sq, rstd = Bf["sq"], Bf["rstd"]
            kb.op("act", lambda: nc.scalar.activation(out=sq[:, :, 0:n], in_=x[:, :, 0:n], func=AF.Square),
                  reads=[xk], writes=["sq"])
            rms_rstd(lambda c: sq[:, c, 0:n], ["sq"] * NCH, n, ones_d)
            for c in range(NCH):
                kb.op("dve", lambda c=c: nc.vector.scalar_tensor_tensor(
                    out=x[:, c, 0:n], in0=x[:, c, 0:n], scalar=vt[:, V_GF + c:V_GF + c + 1], in1=rstd[:, 0:n],
                    op0=ALU.mult, op1=ALU.mult),
                    reads=[xk, "rstd", "vecs"], writes=[xk])
            kb.dma("sp", "st", outT_v[:, :, t0:t0 + n], x[:, :, 0:n], reads=[xk])

        xk = "xt0"
        qS_v = qS.rearrange("h p t -> p h t")
        kS_v = kS.rearrange("h p t -> p h t")
        gaS_v = gaS.rearrange("h p t -> p h t")
        x1T_v = x1T.rearrange("(c p) t -> p c t", p=128)
        if rw_on:
            grS_v = grS.rearrange("h p t -> p h t")
            prS_v = prS.rearrange("c p t -> p c t")
            rwS_v = rwS.rearrange("c p t -> p c t")

        with contextlib.ExitStack() as p1:
            alloc_tile_bufs(p1)
            x, sq, hb, act, rstd = Bf["xt0"], Bf["sq"], Bf["hb"], Bf["act"], Bf["rstd"]
            cs = sbs(p1, "cs", [128, TT], F32)
            sn = sbs(p1, "sn", [128, TT], F32)
            qnb = sbs(p1, "qnb", [128, TT], BF16)
            qsq = sbs(p1, "qsq", [128, TT], BF16)
            vst = sbs(p1, "vst", [128, 512], BF16)
            if rw_on:
                zt = sbs(p1, "zt", [128, 27], F32)
                kb.op("dve", lambda: nc.vector.memset(zt[:], 0.0), writes=["zt"])
                for col in (0, CTX + 1, CTX + 2, CTX + 3 + T):
                    kb.dma("sp", "stz", prS_v[:, :, col:col + 1], zt[:, :].rearrange("p (c o) -> p c o", o=1),
                           reads=["zt"], writes=["hbm_pr"], allow_slow_non_contiguous=True)
            for kind, t0, n in tiles1:
                wh = 1 if kind == "ctx" else 0
                src = cxT_v if kind == "ctx" else xT_v
                kb.dma("sp", "ldx", x[:, :, 0:n], src[:, :, t0:t0 + n], writes=[xk])
                ffn(x, xk, 0, wh, n)
                if kind == "lat":
                    kb.dma("sp", "stx1", x1T_v[:, :, t0:t0 + n], x[:, :, 0:n], reads=[xk], writes=["hbm_x1"])
                    kb.dma("sp", "ldcs", cs[:, 0:n], cosT[:, t0:t0 + n], writes=["cs"])
                    kb.dma("sp", "ldsn", sn[:, 0:n], sinT[:, t0:t0 + n], writes=["sn"])
                rms_modulate(x, xk, 1, wh, n)

                def qk_epi(gcol, rope, dst_j0):
                    def epi(oc, pb, pk):
                        kb.op("act", lambda: nc.scalar.activation(out=qsq[:, 0:n], in_=pb[:, 0:n], func=AF.Square),
                              reads=[pk], writes=["qsq"])
                        rms_rstd(lambda c: qsq[:, 0:n], ["qsq"], n, ones_h)
                        dst = act[:, dst_j0 + oc, 0:n]
                        dk = ("act", dst_j0 + oc)
                        if not rope:
                            kb.op("dve", lambda: nc.vector.scalar_tensor_tensor(
                                out=dst, in0=pb[:, 0:n], scalar=vt[:, gcol:gcol + 1], in1=rstd[:, 0:n],
                                op0=ALU.mult, op1=ALU.mult), reads=[pk, "vecs", "rstd"], writes=[dk])
                            return
                        kb.op("dve", lambda: nc.vector.scalar_tensor_tensor(
                            out=qnb[:, 0:n], in0=pb[:, 0:n], scalar=vt[:, gcol:gcol + 1], in1=rstd[:, 0:n],
                            op0=ALU.mult, op1=ALU.mult), reads=[pk, "vecs", "rstd"], writes=["qnb"])
                        pp, ppk = bank()
                        kb.op("pe", lambda: nc.tensor.matmul(pp[:, 0:n], lhsT=perm_b[:], rhs=qnb[:, 0:n], start=True, stop=True),
                              reads=["perm_b", "qnb"], writes=[ppk])
                        t1, t1k = tmpbuf()
                        t2, t2k = tmpbuf()
                        kb.op("dve", lambda: nc.vector.tensor_tensor(out=t1[:, 0:n], in0=qnb[:, 0:n], in1=cs[:, 0:n], op=ALU.mult),
                              reads=["qnb", "cs"], writes=[t1k])
                        kb.op("dve", lambda: nc.vector.tensor_tensor(out=t2[:, 0:n], in0=pp[:, 0:n], in1=sn[:, 0:n], op=ALU.mult),
                              reads=[ppk, "sn"], writes=[t2k])
                        kb.op("dve", lambda: nc.vector.tensor_tensor(out=dst, in0=t1[:, 0:n], in1=t2[:, 0:n], op=ALU.add),
                              reads=[t1k, t2k], writes=[dk])
                    return epi

                hk = lambda kc: "hb"
                if kind == "lat":
                    linear(hb, hk, n, 16, qk_epi(V_QN, True, 0))
                    kb.dma("sp", "stq", qS_v[:, :, t0:t0 + n], act[:, 0:16, 0:n],
                           reads=[("act", j) for j in range(16)], writes=["hbm_q"])
                linear(hb, hk, n, 4, qk_epi(V_KN, kind == "lat", 16))
                koff = t0 if kind == "lat" else T
                kb.dma("sp", "stk", kS_v[:, :, koff:koff + n], act[:, 16:20, 0:n],
                       reads=[("act", j) for j in range(16, 20)], writes=["hbm_k"])
                wvs = [ws.next() for _ in range(4)]
                for tb_ in range(n // 128):
                    pb, pk = bank()
                    for g in range(4):
                        w, wkey, _ = wvs[g]
                        for kc in range(NCH):
                            kb.op("pe", lambda kc=kc, g=g, w=w: nc.tensor.matmul(
                                pb[:, g * 128:(g + 1) * 128], lhsT=hb[:, kc, tb_ * 128:(tb_ + 1) * 128], rhs=w[:, kc, :],
                                start=(kc == 0), stop=(kc == NCH - 1)),
                                reads=[wkey, "hb"], writes=[pk])
                    kb.op("act", lambda: nc.scalar.copy(out=vst[:, :], in_=pb[:, :]), reads=[pk], writes=["vst"])
                    cblk = (koff + tb_ * 128) // 128
                    kb.dma("sp", "stv", vS[:, :, cblk, :].rearrange("g p e -> p g e"),
                           vst[:, :].rearrange("p (g e) -> p g e", g=4), reads=["vst"], writes=["hbm_v"])
                for g in range(4):
                    ws.done(wvs[g][2])
                if kind == "lat":
                    def ga_epi(oc, pb, pk):
                        kb.op("act", lambda: nc.scalar.activation(out=act[:, 20 + oc, 0:n], in_=pb[:, 0:n], func=AF.Sigmoid),
                              reads=[pk], writes=[("act", 20 + oc)])
                    linear(hb, hk, n, 16, ga_epi)
                    kb.dma("sp", "stga", gaS_v[:, :, t0:t0 + n], act[:, 20:36, 0:n],
                           reads=[("act", j) for j in range(20, 36)], writes=["hbm_ga"])
                if rw_on:
                    pcol = (COL_LAT if kind == "lat" else COL_CTX) + t0

                    def rw_epi(oc, pb, pk):
                        tb, tk = tmpbuf()
                        kb.op("act", lambda: nc.scalar.copy(out=tb[:, 0:n], in_=pb[:, 0:n]), reads=[pk], writes=[tk])
                        kb.dma("sp", f"stp{tk[1]}", prS[oc][:, pcol:pcol + n], tb[:, 0:n], reads=[tk], writes=["hbm_pr"])
                    linear(hb, hk, n, 27, rw_epi)
                    if kind == "lat":
                        def gr_epi(oc, pb, pk):
                            kb.op("act", lambda: nc.scalar.activation(out=act[:, oc, 0:n], in_=pb[:, 0:n], func=AF.Sigmoid),
                                  reads=[pk], writes=[("act", oc)])
                        linear(hb, hk, n, 16, gr_epi)
                        kb.dma("sp", "stgr", grS_v[:, :, t0:t0 + n], act[:, 0:16, 0:n],
                               reads=[("act", j) for j in range(16)], writes=["hbm_gr"])
            barrier()

        if rw_on:
            with contextlib.ExitStack() as pr:
                NROT[0] = 8
                R_ = lambda name, shape, dtype=F32: sbs(pr, name, shape, dtype)
                w2p = R_("w2p", [128, 2, 1024]); a2p = R_("a2p", [128, 2, 1024]); g2t = R_("g2t", [128, 1024])
                w0r = R_("w0r", [1, 2, 1024]); onesr = R_("onesr", [1, 128])
                tri = R_("tri", [128, 4, 128]); mkA = R_("mkA", [128, 2, 512]); mkB = R_("mkB", [128, 2, 128])
                mq = R_("mq", [128, 2, 8, 128])
                kb.dma("sp", "ldc_mq", mq[:, :, :, :], mq_d[:, :, :, :], writes=["mq"])
                ident = R_("ident", [128, 128]); o64 = R_("o64", [64, 64]); o64m = R_("o64m", [64, 64])
                hv = R_("hv", [64, NHV]); hv1 = R_("hv1", [64, 64]); hvh = R_("hvh", [64, 48]); sv = R_("sv", [128, 3])
                sv1 = R_("sv1", [128, 3]); svh = R_("svh", [128, 3]); gne = R_("gne", [64, 1])
                Hs = R_("Hs", [64, 16, 64])
                for tl_, dr_, nm in ((w2p, w2p_d, "w2p"), (a2p, a2p_d, "a2p"), (tri, tri_d, "tri"), (mkA, mkA_d, "mkA"),
                                     (mkB, mkB_d, "mkB"), (w0r, w0r_d, "w0r")):
                    kb.dma("sp", "ldc_" + nm, tl_[:, :, :], dr_[:, :, :], writes=[nm])
                for tl_, dr_, nm in ((g2t, g2_d, "g2t"), (ident, id_d, "ident"), (hv, hvec_d, "hv"), (sv, svec_d, "sv")):
                    kb.dma("sp", "ldc_" + nm, tl_[:, :], dr_[:, :], writes=[nm])
                kb.op("dve", lambda: nc.vector.memset(onesr[:], 1.0), writes=["onesr"])
                kb.op("dve", lambda: nc.vector.memset(o64[:], 1.0), writes=["o64"])
                kb.op("dve", lambda: nc.vector.memset(o64m[:], 1.0 / 64), writes=["o64m"])
                kb.op("dve", lambda: nc.vector.memset(gne[:], GN_EPS), writes=["gne"])
                kb.op("dve", lambda: nc.vector.tensor_scalar(out=hv1[:, 0:48], in0=hv[:, HV_MU:HV_MU + 48], scalar1=-1.0, scalar2=1.0,
                                                            op0=ALU.mult, op1=ALU.add), reads=["hv"], writes=["hv1"])
                kb.op("dve", lambda: nc.vector.tensor_scalar(out=hv1[:, 48:64], in0=hv[:, HV_KA:HV_KA + 16], scalar1=-1.0, scalar2=1.0,
                                                            op0=ALU.mult, op1=ALU.add), reads=["hv"], writes=["hv1"])
                kb.op("dve", lambda: nc.vector.tensor_scalar(out=hvh[:, :], in0=hv[:, HV_MU:HV_MU + 48], scalar1=0.5, scalar2=None,
                                                            op0=ALU.mult), reads=["hv"], writes=["hvh"])
                kb.op("dve", lambda: nc.vector.tensor_scalar(out=sv1[:, :], in0=sv[:, :], scalar1=-1.0, scalar2=1.0,
                                                            op0=ALU.mult, op1=ALU.add), reads=["sv"], writes=["sv1"])
                kb.op("dve", lambda: nc.vector.tensor_scalar(out=svh[:, :], in0=sv[:, :], scalar1=0.5, scalar2=None,
                                                            op0=ALU.mult), reads=["sv"], writes=["svh"])
                shr = R_("shr", [128, 3, CB + 2]); shs = R_("shs", [128, 3, CB]); shq = R_("shq", [128, 3, CB])
                psh = R_("psh", [128, 3, CB]); tlw = R_("tlw", [128, CB]); sgw = R_("sgw", [128, 1024]); sgl = R_("sgl", [128, CB])
                NSET = 2
                S = []
                for i in range(NSET):
                    d = {}
                    d["raw"] = R_(f"raw{i}", [64, 3, CB + 2])
                    for nm in ("s3", "q3", "pm3"):
                        d[nm] = R_(f"{nm}{i}", [64, 3, CB])
                    for nm in ("Ep", "Ex", "Em", "az", "kk", "t1", "kkn", "kmod", "bb", "rt", "at", "bt", "kt", "bh", "kh", "bon",
                               "WT", "y0", "b0", "yv", "yc", "t2"):
                        d[nm] = R_(f"{nm}{i}", [64, CB])
                    d["TM"] = R_(f"TM{i}", [128, 192])
                    d["SA"] = R_(f"SA{i}", [128, 512])
                    d["SB"] = R_(f"SB{i}", [128, 128])
                    d["X"] = R_(f"X{i}", [128, 128])
                    for nm in ("Tm", "TT", "P1s", "tq"):
                        d[nm] = R_(f"{nm}{i}", [128, 128])
                    d["U0"] = R_(f"U0{i}", [128, 64])
                    d["U"] = R_(f"U{i}", [128, 64])
                    d["ob"] = R_(f"ob{i}", [64, CB], BF16)
                    S.append(d)
                prS_h = prS[0:24].rearrange("(g c) p t -> p g c t", g=3)
                rwS_h = rwS.rearrange("c (s p) t -> (c s) p t", s=2)
                unit = [0]

                def head_block(z, h, c0, tok0, emit):
                    i = unit[0] % NSET
                    unit[0] += 1
                    d = S[i]
                    K_ = lambda nm: (nm, i)
                    hc, hs_ = h // 2, h % 2
                    colv = lambda base: hv[:, base + h:base + h + 1]
                    kb.dma("sp", f"ldraw{i}", d["raw"][:, :, :], prS_h[hs_ * 64:(hs_ + 1) * 64, :, hc, c0 - 1:c0 + CB + 1],
                           reads=["hbm_pr"], writes=[K_("raw")])
                    kb.op("dve", lambda: nc.vector.tensor_tensor(out=d["s3"][:, :, :], in0=d["raw"][:, :, 0:CB], in1=d["raw"][:, :, 2:CB + 2], op=ALU.add),
                          reads=[K_("raw")], writes=[K_("s3")])
                    for g in range(3):
                        kb.op("dve", lambda g=g: nc.vector.tensor_scalar(out=d["q3"][:, g, :], in0=d["raw"][:, g, 1:CB + 1],
                                                                        scalar1=hv1[:, g * 16 + h:g * 16 + h + 1], scalar2=None, op0=ALU.mult),
                              reads=[K_("raw"), "hv1"], writes=[K_("q3")])
                        kb.op("dve", lambda g=g: nc.vector.scalar_tensor_tensor(out=d["pm3"][:, g, :], in0=d["s3"][:, g, :],
                                                                               scalar=hvh[:, g * 16 + h:g * 16 + h + 1], in1=d["q3"][:, g, :],
                                                                               op0=ALU.mult, op1=ALU.add),
                              reads=[K_("s3"), K_("q3"), "hvh"], writes=[K_("pm3")])
                    rh, kh_, vh = d["pm3"][:, 0, :], d["pm3"][:, 1, :], d["pm3"][:, 2, :]
                    pb, pk = bank()
                    kb.op("pe", lambda: nc.tensor.matmul(pb[0:64, 0:CB], lhsT=sgw[:, h * 64:(h + 1) * 64], rhs=tri[:, 2 * z, :], start=True, stop=True),
                          reads=["sgw", "tri"], writes=[pk])
                    kb.op("pe", lambda: nc.tensor.matmul(pb[0:64, CB:2 * CB], lhsT=sgw[:, h * 64:(h + 1) * 64], rhs=tri[:, 2 * z + 1, :], start=True, stop=True),
                          reads=["sgw", "tri"], writes=[pk])
                    kb.op("act", lambda: nc.scalar.activation(out=d["Ep"][:, :], in_=pb[0:64, 0:CB], func=AF.Exp), reads=[pk], writes=[K_("Ep")])
                    kb.op("act", lambda: nc.scalar.activation(out=d["Em"][:, :], in_=pb[0:64, 0:CB], func=AF.Exp, scale=-1.0), reads=[pk], writes=[K_("Em")])
                    kb.op("act", lambda: nc.scalar.activation(out=d["Ex"][:, :], in_=pb[0:64, CB:2 * CB], func=AF.Exp), reads=[pk], writes=[K_("Ex")])
                    pcl = CB - 1 if z == 0 else 0
                    PC = d["Ep"][:, pcl:pcl + 1]
                    pa, pak = bank()
                    kb.op("pe", lambda: nc.tensor.matmul(pa[0:64, 0:CB], lhsT=a2p[:, z, h * 64:(h + 1) * 64], rhs=psh[:, 1, :], start=True, stop=True),
                          reads=["a2p", "psh"], writes=[pak])
                    kb.op("act", lambda: nc.scalar.activation(out=d["az"][:, :], in_=pa[0:64, 0:CB], func=AF.Sigmoid,
                                                             bias=hv[:, HV_A0 + z * 16 + h:HV_A0 + z * 16 + h + 1], scale=1.0),
                          reads=[pak, "hv"], writes=[K_("az")])
                    kb.op("dve", lambda: nc.vector.tensor_scalar(out=d["kk"][:, :], in0=kh_, scalar1=colv(HV_KK), scalar2=None, op0=ALU.mult),
                          reads=[K_("pm3"), "hv"], writes=[K_("kk")])
                    kb.op("dve", lambda: nc.vector.tensor_tensor(out=d["t1"][:, :], in0=d["kk"][:, :], in1=d["kk"][:, :], op=ALU.mult),
                          reads=[K_("kk")], writes=[K_("t1")])
                    pn_, pnk = bank()
                    kb.op("pe", lambda: nc.tensor.matmul(pn_[0:64, 0:CB], lhsT=o64[:, :], rhs=d["t1"][:, :], start=True, stop=True),
                          reads=["o64", K_("t1")], writes=[pnk])
                    kb.op("act", lambda: nc.scalar.activation(out=d["t2"][:, :], in_=pn_[0:64, 0:CB], func=AF.Sqrt), reads=[pnk], writes=[K_("t2")])
                    kb.op("dve", lambda: nc.vector.tensor_scalar(out=d["t2"][:, :], in0=d["t2"][:, :], scalar1=1e-12, scalar2=None, op0=ALU.max),
                          reads=[K_("t2")], writes=[K_("t2")])
                    kb.op("dve", lambda: nc.vector.reciprocal(out=d["t2"][:, :], in_=d["t2"][:, :]), reads=[K_("t2")], writes=[K_("t2")])
                    kb.op("dve", lambda: nc.vector.tensor_tensor(out=d["kkn"][:, :], in0=d["kk"][:, :], in1=d["t2"][:, :], op=ALU.mult),
                          reads=[K_("kk"), K_("t2")], writes=[K_("kkn")])
                    kb.op("dve", lambda: nc.vector.tensor_scalar(out=d["t1"][:, :], in0=d["az"][:, :], scalar1=colv(HV_KA), scalar2=hv1[:, 48 + h:49 + h],
                                                                op0=ALU.mult, op1=ALU.add), reads=[K_("az"), "hv", "hv1"], writes=[K_("t1")])
                    kb.op("dve", lambda: nc.vector.tensor_tensor(out=d["kmod"][:, :], in0=kh_, in1=d["t1"][:, :], op=ALU.mult),
                          reads=[K_("pm3"), K_("t1")], writes=[K_("kmod")])
                    kb.op("dve", lambda: nc.vector.tensor_tensor(out=d["bb"][:, :], in0=d["kkn"][:, :], in1=d["az"][:, :], op=ALU.mult),
                          reads=[K_("kkn"), K_("az")], writes=[K_("bb")])
                    kb.op("dve", lambda: nc.vector.tensor_tensor(out=d["rt"][:, :], in0=rh, in1=d["Ep"][:, :], op=ALU.mult),
                          reads=[K_("pm3"), K_("Ep")], writes=[K_("rt")])
                    kb.op("dve", lambda: nc.vector.scalar_tensor_tensor(out=d["at"][:, :], in0=d["kkn"][:, :], scalar=-1.0, in1=d["Ex"][:, :],
                                                                       op0=ALU.mult, op1=ALU.mult), reads=[K_("kkn"), K_("Ex")], writes=[K_("at")])
                    kb.op("dve", lambda: nc.vector.tensor_tensor(out=d["bt"][:, :], in0=d["bb"][:, :], in1=d["Em"][:, :], op=ALU.mult),
                          reads=[K_("bb"), K_("Em")], writes=[K_("bt")])
                    kb.op("dve", lambda: nc.vector.tensor_tensor(out=d["kt"][:, :], in0=d["kmod"][:, :], in1=d["Em"][:, :], op=ALU.mult),
                          reads=[K_("kmod"), K_("Em")], writes=[K_("kt")])
                    kb.op("dve", lambda: nc.vector.tensor_scalar(out=d["bh"][:, :], in0=d["bt"][:, :], scalar1=PC, scalar2=None, op0=ALU.mult),
                          reads=[K_("bt"), K_("Ep")], writes=[K_("bh")])
                    kb.op("dve", lambda: nc.vector.tensor_scalar(out=d["kh"][:, :], in0=d["kt"][:, :], scalar1=PC, scalar2=None, op0=ALU.mult),
                          reads=[K_("kt"), K_("Ep")], writes=[K_("kh")])
                    if emit:
                        kb.op("dve", lambda: nc.vector.scalar_tensor_tensor(out=d["t1"][:, :], in0=rh, scalar=colv(HV_RK), in1=d["kmod"][:, :],
                                                                           op0=ALU.mult, op1=ALU.mult), reads=[K_("pm3"), K_("kmod"), "hv"], writes=[K_("t1")])
                        pbn, pbnk = bank()
                        kb.op("pe", lambda: nc.tensor.matmul(pbn[0:64, 0:CB], lhsT=o64[:, :], rhs=d["t1"][:, :], start=True, stop=True),
                              reads=["o64", K_("t1")], writes=[pbnk])
                        kb.op("dve", lambda: nc.vector.tensor_tensor(out=d["bon"][:, :], in0=pbn[0:64, 0:CB], in1=vh, op=ALU.mult),
                              reads=[pbnk, K_("pm3")], writes=[K_("bon")])
                    ptm, ptmk = bank()
                    for j, (srcnm, sap) in enumerate((("bh", d["bh"][:, :]), ("kh", d["kh"][:, :]), ("pm3", vh), ("at", d["at"][:, :]))):
                        kb.op("pe", lambda j=j, sap=sap: nc.tensor.matmul(ptm[:, j * 64:(j + 1) * 64], lhsT=sap, rhs=ident[0:64, 0:64], start=True, stop=True),
                              reads=[K_(srcnm), "ident"], writes=[ptmk])
                    kb.op("act", lambda: nc.scalar.copy(out=d["TM"][:, :], in_=ptm[:, 0:192]), reads=[ptmk], writes=[K_("TM")])
                    kb.op("act", lambda: nc.scalar.copy(out=d["X"][:, 0:64], in_=ptm[:, 192:256]), reads=[ptmk], writes=[K_("X")])
                    bhT, khT, VT = d["TM"][:, 0:64], d["TM"][:, 64:128], d["TM"][:, 128:192]
                    psa, psak = bank()
                    psb, psbk = bank()
                    for j, (l_, r_) in enumerate((("bt", "at"), ("at", "bt"), ("kt", "at"), ("bt", "rt"))):
                        kb.op("pe", lambda j=j, l_=l_, r_=r_: nc.tensor.matmul(psa[:, j * CB:(j + 1) * CB], lhsT=d[l_][:, :], rhs=d[r_][:, :], start=True, stop=True),
                              reads=[K_(l_), K_(r_)], writes=[psak])
                    kb.op("pe", lambda: nc.tensor.matmul(psb[:, 0:CB], lhsT=d["kt"][:, :], rhs=d["rt"][:, :], start=True, stop=True),
                          reads=[K_("kt"), K_("rt")], writes=[psbk])
                    kb.op("dve", lambda: nc.vector.tensor_tensor(out=d["SA"][:, :], in0=psa[:, :], in1=mkA[:, z, :], op=ALU.mult),
                          reads=[psak, "mkA"], writes=[K_("SA")])
                    kb.op("dve", lambda: nc.vector.tensor_tensor(out=d["SB"][:, :], in0=psb[:, 0:CB], in1=mkB[:, z, :], op=ALU.mult),
                          reads=[psbk, "mkB"], writes=[K_("SB")])
                    LakT, MrbT, MrkT = d["SA"][:, 2 * CB:3 * CB], d["SA"][:, 3 * CB:4 * CB], d["SB"][:, :]
                    plv, plvk = bank()
                    kb.op("pe", lambda: nc.tensor.matmul(plv[:, 0:64], lhsT=LakT, rhs=VT, start=True, stop=True),
                          reads=[K_("SA"), K_("TM")], writes=[plvk])
                    kb.op("act", lambda: nc.scalar.copy(out=d["X"][:, 64:128], in_=plv[:, 0:64]), reads=[plvk], writes=[K_("X")])
                    LabT_, Lab_ = d["SA"][:, 0:CB], d["SA"][:, CB:2 * CB]
                    Tm, TTm, P1s, tq = d["Tm"], d["TT"], d["P1s"], d["tq"]
                    kb.op("dve", lambda: nc.vector.tensor_tensor(out=Tm[:, :], in0=Lab_, in1=mq[:, z, 0, :], op=ALU.mult),
                          reads=[K_("SA"), "mq"], writes=[K_("Tm")])
                    kb.op("dve", lambda: nc.vector.tensor_tensor(out=Tm[:, :], in0=Tm[:, :], in1=ident[:, :], op=ALU.add),
                          reads=[K_("Tm"), "ident"], writes=[K_("Tm")])
                    kb.op("dve", lambda: nc.vector.tensor_tensor(out=TTm[:, :], in0=LabT_, in1=mq[:, z, 7, :], op=ALU.mult),
                          reads=[K_("SA"), "mq"], writes=[K_("TT")])
                    kb.op("dve", lambda: nc.vector.tensor_tensor(out=TTm[:, :], in0=TTm[:, :], in1=ident[:, :], op=ALU.add),
                          reads=[K_("TT"), "ident"], writes=[K_("TT")])
                    for lv in range(1, 7):
                        p1, p1k = bank()
                        kb.op("pe", lambda: nc.tensor.matmul(p1[:, 0:CB], lhsT=LabT_, rhs=Tm[:, :], start=True, stop=True),
                              reads=[K_("SA"), K_("Tm")], writes=[p1k])
                        kb.op("act", lambda: nc.scalar.copy(out=P1s[:, :], in_=p1[:, 0:CB]), reads=[p1k], writes=[K_("P1s")])
                        p2, p2k = bank()
                        kb.op("pe", lambda: nc.tensor.matmul(p2[:, 0:CB], lhsT=TTm[:, :], rhs=P1s[:, :], start=True, stop=True),
                              reads=[K_("TT"), K_("P1s")], writes=[p2k])
                        kb.op("dve", lambda lv=lv: nc.vector.tensor_tensor(out=tq[:, :], in0=p2[:, 0:CB], in1=mq[:, z, lv, :], op=ALU.mult),
                              reads=[p2k, "mq"], writes=[K_("tq")])
                        kb.op("dve", lambda: nc.vector.tensor_tensor(out=Tm[:, :], in0=Tm[:, :], in1=tq[:, :], op=ALU.add),
                              reads=[K_("Tm"), K_("tq")], writes=[K_("Tm")])
                        p3, p3k = bank()
                        kb.op("pe", lambda: nc.tensor.matmul(p3[:, 0:CB], lhsT=tq[:, :], rhs=ident[:, :], start=True, stop=True),
                              reads=[K_("tq"), "ident"], writes=[p3k])
                        kb.op("dve", lambda: nc.vector.tensor_tensor(out=TTm[:, :], in0=TTm[:, :], in1=p3[:, 0:CB], op=ALU.add),
                              reads=[K_("TT"), p3k], writes=[K_("TT")])
                    pw, pwk = bank()
                    kb.op("pe", lambda: nc.tensor.matmul(pw[0:64, 0:CB], lhsT=d["X"][:, 0:64], rhs=TTm[:, :], start=True, stop=True),
                          reads=[K_("X"), K_("TT")], writes=[pwk])
                    kb.op("act", lambda: nc.scalar.copy(out=d["WT"][:, :], in_=pw[0:64, 0:CB]), reads=[pwk], writes=[K_("WT")])
                    pu0, pu0k = bank()
                    kb.op("pe", lambda: nc.tensor.matmul(pu0[:, 0:64], lhsT=TTm[:, :], rhs=d["X"][:, 64:128], start=True, stop=True),
                          reads=[K_("TT"), K_("X")], writes=[pu0k])
                    kb.op("act", lambda: nc.scalar.copy(out=d["U0"][:, :], in_=pu0[:, 0:64]), reads=[pu0k], writes=[K_("U0")])
                    Hh = Hs[:, h, :]
                    Hk = ("H", h)
                    pu_, puk_ = bank()
                    kb.op("pe", lambda: nc.tensor.matmul(pu_[:, 0:64], lhsT=d["WT"][:, :], rhs=Hh, start=True, stop=True),
                          reads=[K_("WT"), Hk], writes=[puk_])
                    kb.op("dve", lambda: nc.vector.tensor_tensor(out=d["U"][:, :], in0=d["U0"][:, :], in1=pu_[:, 0:64], op=ALU.add),
                          reads=[K_("U0"), puk_], writes=[K_("U")])
                    if emit:
                        py, pyk = bank()
                        kb.op("pe", lambda: nc.tensor.matmul(py[0:64, 0:CB], lhsT=Hh, rhs=d["rt"][:, :], start=True, stop=True),
                              reads=[Hk, K_("rt")], writes=[pyk])
                        kb.op("pe", lambda: nc.tensor.matmul(py[0:64, CB:2 * CB], lhsT=d["U"][:, :], rhs=MrbT, start=True, stop=False),
                              reads=[K_("U"), K_("SA")], writes=[pyk])
                        kb.op("pe", lambda: nc.tensor.matmul(py[0:64, CB:2 * CB], lhsT=VT, rhs=MrkT, start=False, stop=True),
                              reads=[K_("TM"), K_("SB")], writes=[pyk])
                    ph, phk = bank()
                    kb.op("pe", lambda: nc.tensor.matmul(ph[0:64, 0:64], lhsT=bhT, rhs=d["U"][:, :], start=True, stop=False),
                          reads=[K_("TM"), K_("U")], writes=[phk])
                    kb.op("pe", lambda: nc.tensor.matmul(ph[0:64, 0:64], lhsT=khT, rhs=VT, start=False, stop=True),
                          reads=[K_("TM")], writes=[phk])
                    kb.op("dve", lambda: nc.vector.scalar_tensor_tensor(out=Hh, in0=Hh, scalar=PC, in1=ph[0:64, 0:64], op0=ALU.mult, op1=ALU.add),
                          reads=[Hk, K_("Ep"), phk], writes=[Hk])
                    if cfg.get("debug") and z == 0 and (not emit) and c0 == COL_CTX and h < 2:
                        def dump(slot, ap, key, P, F_):
                            kb.dma("sp", f"dbg{slot}", dbgT[h, slot][0:P, 0:F_], ap, reads=[key])
                        dump(0, d["Ep"][:, :], K_("Ep"), 64, CB)
                        dump(1, d["at"][:, :], K_("at"), 64, CB)
                        dump(2, d["bt"][:, :], K_("bt"), 64, CB)
                        dump(3, d["kt"][:, :], K_("kt"), 64, CB)
                        dump(4, d["rt"][:, :], K_("rt"), 64, CB)
                        dump(5, d["SA"][:, :], K_("SA"), 128, 512)
                        dump(6, d["Tm"][:, :], K_("Tm"), 128, 128)
                        dump(7, d["U"][:, :], K_("U"), 128, 64)
                        dump(8, Hh, Hk, 64, 64)
                        dump(9, d["TM"][:, :], K_("TM"), 128, 192)
                        dump(10, d["SB"][:, :], K_("SB"), 128, 128)
                        dump(11, sgw[:, h * 64:(h + 1) * 64], "sgw", 128, 64)
                    if not emit:
                        return
                    if z == 0:
                        kb.op("act", lambda: nc.scalar.copy(out=d["yv"][:, :], in_=py[0:64, 0:CB]), reads=[pyk], writes=[K_("yv")])
                        kb.op("dve", lambda: nc.vector.tensor_tensor(out=d["yv"][:, :], in0=d["yv"][:, :], in1=py[0:64, CB:2 * CB], op=ALU.add),
                              reads=[K_("yv"), pyk], writes=[K_("yv")])
                        kb.dma("sp", f"sty{i}", yS[h][:, tok0:tok0 + CB], d["yv"][:, :], reads=[K_("yv")], writes=["hbm_y"])
                        kb.dma("sp", f"stb{i}", bS[h][:, tok0:tok0 + CB], d["bon"][:, :], reads=[K_("bon")], writes=["hbm_b"])
                        return
                    kb.dma("sp", f"ldy{i}", d["y0"][:, :], yS[h][:, tok0:tok0 + CB], reads=["hbm_y"], writes=[K_("y0")])
                    kb.dma("sp", f"ldb{i}", d["b0"][:, :], bS[h][:, tok0:tok0 + CB], reads=["hbm_b"], writes=[K_("b0")])
                    kb.op("dve", lambda: nc.vector.tensor_tensor(out=d["yv"][:, :], in0=d["y0"][:, :], in1=py[0:64, 0:CB], op=ALU.add),
                          reads=[K_("y0"), pyk], writes=[K_("yv")])
                    kb.op("dve", lambda: nc.vector.tensor_tensor(out=d["yv"][:, :], in0=d["yv"][:, :], in1=py[0:64, CB:2 * CB], op=ALU.add),
                          reads=[K_("yv"), pyk], writes=[K_("yv")])
                    pmn, pmnk = bank()
                    kb.op("pe", lambda: nc.tensor.matmul(pmn[0:64, 0:CB], lhsT=o64m[:, :], rhs=d["yv"][:, :], start=True, stop=True),
                          reads=["o64m", K_("yv")], writes=[pmnk])
                    kb.op("dve", lambda: nc.vector.tensor_tensor(out=d["yc"][:, :], in0=d["yv"][:, :], in1=pmn[0:64, 0:CB], op=ALU.subtract),
                          reads=[K_("yv"), pmnk], writes=[K_("yc")])
                    kb.op("dve", lambda: nc.vector.tensor_tensor(out=d["t1"][:, :], in0=d["yc"][:, :], in1=d["yc"][:, :], op=ALU.mult),
                          reads=[K_("yc")], writes=[K_("t1")])
                    pvr, pvrk = bank()
                    kb.op("pe", lambda: nc.tensor.matmul(pvr[0:64, 0:CB], lhsT=o64m[:, :], rhs=d["t1"][:, :], start=True, stop=True),
                          reads=["o64m", K_("t1")], writes=[pvrk])
                    kb.op("act", lambda: nc.scalar.activation(out=d["t2"][:, :], in_=pvr[0:64, 0:CB], func=AF.Sqrt, bias=gne[:], scale=1.0),
                          reads=[pvrk, "gne"], writes=[K_("t2")])
                    kb.op("dve", lambda: nc.vector.reciprocal(out=d["t2"][:, :], in_=d["t2"][:, :]), reads=[K_("t2")], writes=[K_("t2")])
                    kb.op("dve", lambda: nc.vector.tensor_tensor(out=d["yc"][:, :], in0=d["yc"][:, :], in1=d["t2"][:, :], op=ALU.mult),
                          reads=[K_("yc"), K_("t2")], writes=[K_("yc")])
                    kb.op("dve", lambda: nc.vector.tensor_scalar(out=d["yc"][:, :], in0=d["yc"][:, :], scalar1=colv(HV_LW), scalar2=colv(HV_LB),
                                                                op0=ALU.mult, op1=ALU.add), reads=[K_("yc"), "hv"], writes=[K_("yc")])
                    kb.op("dve", lambda: nc.vector.tensor_tensor(out=d["b0"][:, :], in0=d["b0"][:, :], in1=d["bon"][:, :], op=ALU.add),
                          reads=[K_("b0"), K_("bon")], writes=[K_("b0")])
                    kb.op("dve", lambda: nc.vector.tensor_tensor(out=d["yc"][:, :], in0=d["yc"][:, :], in1=d["b0"][:, :], op=ALU.add),
                          reads=[K_("yc"), K_("b0")], writes=[K_("yc")])
                    pg_, pgk_ = bank()
                    kb.op("pe", lambda: nc.tensor.matmul(pg_[0:64, 0:CB], lhsT=g2t[:, h * 64:(h + 1) * 64], rhs=sgl[:, :], start=True, stop=True),
                          reads=["g2t", "sgl"], writes=[pgk_])
                    kb.op("dve", lambda: nc.vector.tensor_tensor(out=d["ob"][:, :], in0=d["yc"][:, :], in1=pg_[0:64, 0:CB], op=ALU.mult),
                          reads=[K_("yc"), pgk_], writes=[K_("ob")])
                    kb.dma("sp", f"strw{i}", rwS_h[h][:, tok0:tok0 + CB], d["ob"][:, :], reads=[K_("ob")], writes=["hbm_rw"])

                for z in range(2):
                    kb.op("dve", lambda: nc.vector.memset(Hs[:, :, :], 0.0), reads=[("H", h) for h in range(16)],
                          writes=[("H", h) for h in range(16)])
                    for kind, L, cbase, emit in (("ctx", CTX, COL_CTX, False), ("lat", T, COL_LAT, True)):
                        nb = L // CB
                        blocks = list(range(nb)) if z == 0 else list(range(nb - 1, -1, -1))
                        for blk in blocks:
                            c0 = cbase + blk * CB
                            tok0 = blk * CB
                            kb.dma("sp", "ldshr", shr[:, :, :], prS_v[:, 24:27, c0 - 1:c0 + CB + 1], reads=["hbm_pr"], writes=["shr"])
                            kb.op("dve", lambda: nc.vector.tensor_tensor(out=shs[:, :, :], in0=shr[:, :, 0:CB], in1=shr[:, :, 2:CB + 2], op=ALU.add),
                                  reads=["shr"], writes=["shs"])
                            for g in range(3):
                                kb.op("dve", lambda g=g: nc.vector.tensor_scalar(out=shq[:, g, :], in0=shr[:, g, 1:CB + 1], scalar1=sv1[:, g:g + 1],
                                                                                scalar2=None, op0=ALU.mult), reads=["shr", "sv1"], writes=["shq"])
                                kb.op("dve", lambda g=g: nc.vector.scalar_tensor_tensor(out=psh[:, g, :], in0=shs[:, g, :], scalar=svh[:, g:g + 1],
                                                                                       in1=shq[:, g, :], op0=ALU.mult, op1=ALU.add),
                                      reads=["shs", "shq", "svh"], writes=["psh"])
                            kb.op("act", lambda: nc.scalar.activation(out=tlw[:, :], in_=psh[:, 0, :], func=AF.Tanh), reads=["psh"], writes=["tlw"])
                            for half in range(2):
                                pwr, pwrk = bank()
                                kb.op("pe", lambda half=half: nc.tensor.matmul(pwr[:, :], lhsT=tlw[:, :], rhs=w2p[:, z, half * 512:(half + 1) * 512],
                                                                              start=True, stop=False), reads=["tlw", "w2p"], writes=[pwrk])
                                kb.op("pe", lambda half=half: nc.tensor.matmul(pwr[:, :], lhsT=onesr[0:1, :], rhs=w0r[0:1, z, half * 512:(half + 1) * 512],
                                                                              start=False, stop=True), reads=["onesr", "w0r"], writes=[pwrk])
                                kb.op("act", lambda half=half: nc.scalar.activation(out=sgw[:, half * 512:(half + 1) * 512], in_=pwr[:, :], func=AF.Sigmoid),
                                      reads=[pwrk], writes=["sgw"])
                            if emit and z == 1:
                                kb.op("act", lambda: nc.scalar.activation(out=sgl[:, :], in_=psh[:, 2, :], func=AF.Sigmoid), reads=["psh"], writes=["sgl"])
                            for h in range(16):
                                head_block(z, h, c0, tok0, emit)
                barrier()
                NROT[0] = 6

        with contextlib.ExitStack() as p2:
            alloc_tile_bufs(p2)
            x, sq, hb, act, rstd = Bf["xt0"], Bf["sq"], Bf["hb"], Bf["act"], Bf["rstd"]
            kT_sb = sbs(p2, "kT_sb", [128, NK], BF16)
            v_sb = sbs(p2, "v_sb", [128, NKC, 128], BF16)
            q_sb = [sbs(p2, f"q_sb{i}", [128, TT], BF16) for i in range(2)]
            pT = [sbs(p2, f"pT{i}", [128, TT], BF16) for i in range(3)]
            if rw_on:
                rw_sb = sbs(p2, "rw_sb", [128, 8, TT], BF16)
            po, pok = banks[6], ("bank", 6)
            pd, pdk = banks[7], ("bank", 7)
            at = sq
            n = TT
            for it in range(NT):
                t0 = it * TT
                cnt = 0
                for g in range(NKV):
                    kb.dma("sp", "ldk", kT_sb[:, :], kS[g], reads=["hbm_k"], writes=["kT_sb"])
                    kb.dma("sp", "ldv", v_sb[:, :, :], vS[g], reads=["hbm_v"], writes=["v_sb"])
                    for hh in range(4):
                        h = g * 4 + hh
                        qb = q_sb[h % 2]
                        qk_ = ("q_sb", h % 2)
                        kb.dma("sp", f"ldq{h % 2}", qb[:, 0:n], qS[h][:, t0:t0 + n], reads=["hbm_q"], writes=[qk_])
                        for kc in range(NKC):
                            pb, pk = bank()
                            kb.op("pe", lambda kc=kc: nc.tensor.matmul(pb[:, 0:n], lhsT=kT_sb[:, kc * 128:(kc + 1) * 128],
                                                                      rhs=qb[:, 0:n], start=True, stop=True),
                                  reads=["kT_sb", qk_], writes=[pk])
                            p_ = pT[cnt % 3]
                            p_k = ("pT", cnt % 3)
                            cnt += 1
                            kb.op("act", lambda p_=p_: nc.scalar.activation(out=p_[:, 0:n], in_=pb[:, 0:n], func=AF.Exp, scale=ATT_SCALE),
                                  reads=[pk], writes=[p_k])
                            kb.op("pe", lambda kc=kc, p_=p_: nc.tensor.matmul(po[:, 0:n], lhsT=v_sb[:, kc, :], rhs=p_[:, 0:n],
                                                                             start=(kc == 0), stop=(kc == NKC - 1)),
                                  reads=["v_sb", p_k], writes=[pok])
                            kb.op("pe", lambda kc=kc, p_=p_: nc.tensor.matmul(pd[:, 0:n], lhsT=ones_1[:], rhs=p_[:, 0:n],
                                                                             start=(kc == 0), stop=(kc == NKC - 1)),
                                  reads=["ones", p_k], writes=[pdk])
                        kb.op("dve", lambda: nc.vector.reciprocal(out=rstd[:, 0:n], in_=pd[:, 0:n]), reads=[pdk], writes=["rstd"])
                        kb.op("dve", lambda h=h: nc.vector.tensor_tensor(out=at[:, h, 0:n], in0=po[:, 0:n], in1=rstd[:, 0:n], op=ALU.mult),
                              reads=[pok, "rstd"], writes=["sq"])
                kb.dma("sp", "ldga", act[:, 0:16, 0:n], gaS_v[:, :, t0:t0 + n], reads=["hbm_ga"],
                       writes=[("act", j) for j in range(16)])
                kb.dma("sp", "ldx", x[:, :, 0:n], x1T_v[:, :, t0:t0 + n], reads=["hbm_x1"], writes=[xk])
                if rw_on:
                    kb.dma("sp", "ldgr", hb[:, :, 0:n], grS_v[:, :, t0:t0 + n], reads=["hbm_gr"], writes=["hb"])
                    kb.dma("sp", "ldrw", rw_sb[:, :, 0:n], rwS_v[:, :, t0:t0 + n], reads=["hbm_rw"], writes=["rw_sb"])

                def oa_epi(oc, pb, pk):
                    kb.op("dve", lambda: nc.vector.tensor_tensor(out=act[:, 16 + oc, 0:n], in0=pb[:, 0:n], in1=act[:, oc, 0:n], op=ALU.mult),
                          reads=[pk, ("act", oc)], writes=[("act", 16 + oc)])
                linear(at, lambda kc: "sq", n, 16, oa_epi)
                if rw_on:
                    def or_epi(oc, pb, pk):
                        tb, tk = tmpbuf()
                        kb.op("dve", lambda: nc.vector.tensor_tensor(out=tb[:, 0:n], in0=pb[:, 0:n], in1=hb[:, oc, 0:n], op=ALU.mult),
                              reads=[pk, "hb"], writes=[tk])
                        kb.op("dve", lambda: nc.vector.tensor_tensor(out=act[:, 16 + oc, 0:n], in0=act[:, 16 + oc, 0:n], in1=tb[:, 0:n], op=ALU.add),
                              reads=[tk, ("act", 16 + oc)], writes=[("act", 16 + oc)])
                    linear(rw_sb, lambda kc: "rw_sb", n, 16, or_epi, kcn=8)
                m_v = act[:, 16:32, :]

                def out_epi(oc, pb, pk):
                    kb.op("dve", lambda: nc.vector.scalar_tensor_tensor(
                        out=x[:, oc, 0:n], in0=pb[:, 0:n], scalar=mG[:, 1, oc, 0:1], in1=x[:, oc, 0:n],
                        op0=ALU.mult, op1=ALU.add), reads=[pk, "mG", xk], writes=[xk])
                linear(m_v, lambda kc: ("act", 16 + kc), n, 16, out_epi)
                ffn(x, xk, 2, 0, n)
                final_norm_store(x, xk, t0, n)
            kb._wait("sp", ("st", kb.cnt["st"]))
        assert ws.used == len(plan), (ws.used, len(plan))
        print(f"[build] instructions={kb.n_inst} waits={kb.n_wait}")
    return nc


def host_prep(inputs, b):
    x = np.asarray(inputs["x"], np.float32)
    v = np.zeros((128, NV), np.float32)
    cl = fm_vec(inputs["c"][b])
    cc = fm_vec(inputs["c_ctx"])
    v[:, V_C:V_C + 32:2] = cl
    v[:, V_C + 1:V_C + 32:2] = cc
    v[:, V_BMOD:V_BMOD + 144] = fm_vec(inputs["b_mod"][0])
    v[:, V_G1:V_G1 + 16] = fm_vec(inputs["norm_ffn1"][0])
    v[:, V_GMIX:V_GMIX + 16] = fm_vec(inputs["norm_mix"][0])
    v[:, V_G2:V_G2 + 16] = fm_vec(inputs["norm_ffn2"][0])
    v[:, V_GF:V_GF + 16] = fm_vec(inputs["norm_final"])
    v[:, V_QN] = inputs["q_norm"][0]
    v[:, V_KN] = inputs["k_norm"][0]
    return {"xT": np.ascontiguousarray(x[b].T), "cxT": np.ascontiguousarray(np.asarray(inputs["ctx"], np.float32)[b].T), "vecs": v}


def host_prep_rwkv(inputs):
    f = lambda k: np.asarray(inputs[k], np.float32)[0]
    w_in = f("w_in")
    o_rw = 3072
    o_gr = 3072 + 3456 + 2048
    mu = f("rwkv_mu")
    hv = np.zeros((64, NHV), np.float32)
    for g in range(3):
        hv[:, HV_MU + g * 16:HV_MU + (g + 1) * 16] = head_fm(mu[g * 1024:(g + 1) * 1024])
    hv[:, HV_KK:HV_KK + 16] = head_fm(f("k_k"))
    hv[:, HV_KA:HV_KA + 16] = head_fm(f("k_a"))
    a0 = f("a0")
    hv[:, HV_A0:HV_A0 + 16] = head_fm(a0[0])
    hv[:, HV_A0 + 16:HV_A0 + 32] = head_fm(a0[1])
    hv[:, HV_RK:HV_RK + 16] = head_fm(f("r_k").reshape(-1))
    hv[:, HV_LW:HV_LW + 16] = head_fm(f("ln_x_w"))
    hv[:, HV_LB:HV_LB + 16] = head_fm(f("ln_x_b"))
    sv = np.ascontiguousarray(mu[3072:3456].reshape(3, 128).T)
    w2 = f("w2")
    a2 = f("a2")
    w2p = np.zeros((128, 2, 1024), np.float32)
    a2p = np.zeros((128, 2, 1024), np.float32)
    for z in range(2):
        w2p[z * 64:(z + 1) * 64, z, :] = w2[z]
        a2p[z * 64:(z + 1) * 64, z, :] = a2[z]
    tri, mkA, mkB, ident, mq = rwkv_consts()
    return {
        "wrw": prep_w(w_in[:, o_rw:o_rw + 3456]),
        "wgr": prep_w(w_in[:, o_gr:o_gr + 2048]),
        "wor": prep_w(f("w_or")),
        "w2p": w2p, "a2p": a2p, "g2": np.ascontiguousarray(f("g2")),
        "w0r": np.ascontiguousarray(f("w0").reshape(1, 2, 1024)),
        "hvec": hv, "svec": sv, "tri": tri, "mkA": mkA, "mkB": mkB, "ident": ident, "mq": mq,
    }


def kernel(**inputs):
    inputs = {k: np.asarray(v) for k, v in inputs.items()}
    stages = str(inputs.pop("_stages", "full"))
    ncores = int(inputs.pop("_ncores", 8))
    B, T, _ = inputs["x"].shape
    debug = bool(inputs.pop("_debug", False))
    cfg = {"T": T, "stages": stages, "debug": debug}
    shared = {
        "wmod": prep_w(inputs["w_mod"][0]),
        "w1i": prep_w(inputs["ffn1_w_in"][0]),
        "w1o": prep_w(inputs["ffn1_w_out"][0]),
    }
    if stages != "ffn1":
        w_in = inputs["w_in"][0]
        o_q, o_k, o_v, o_rw, o_ga = 0, 2048, 2560, 3072, 3072 + 3456
        cosT, sinT, perm = rope_tables(T)
        shared.update({
            "w2i": prep_w(inputs["ffn2_w_in"][0]),
            "w2o": prep_w(inputs["ffn2_w_out"][0]),
            "wq": prep_w(w_in[:, o_q:o_k]),
            "wk": prep_w(w_in[:, o_k:o_v]),
            "wv": prep_w(w_in[:, o_v:o_rw]),
            "wga": prep_w(w_in[:, o_ga:o_ga + 2048]),
            "woa": prep_w(inputs["w_oa"][0]),
            "wout": prep_w(inputs["w_out"][0]),
            "cosT": cosT, "sinT": sinT, "perm": perm,
        })
    if stages == "full":
        shared.update(host_prep_rwkv(inputs))
    nc = build_program(cfg)
    in_maps = []
    for core in range(ncores):
        b = core % B
        m = dict(shared)
        m.update(host_prep(inputs, b))
        in_maps.append(m)
    res = run_bass_kernel_spmd(nc, in_maps, core_ids=list(range(ncores)))
    if debug:
        return res.results[0]
    out = np.stack([np.ascontiguousarray(res.results[b % ncores]["outT"].T) for b in range(B)], axis=0)
    return out.astype(np.float32)
```

```python
import contextlib
import numpy as np
import concourse.bass as bass
import concourse.mybir as mybir
from concourse.bass_utils import run_bass_kernel_spmd

F32 = mybir.dt.float32
BF16 = mybir.dt.bfloat16
AF = mybir.ActivationFunctionType
ALU = mybir.AluOpType

D = 2048
NCH = 16
DFF = 5632
NJ = 44
NMOD = 9
EPS = 1e-6


class Res:
    __slots__ = ("w", "r", "name")

    def __init__(self, name):
        self.name = name
        self.w = None
        self.r = {}


class KB:
    def __init__(self, nc, es):
        self.nc = nc
        self.es = es
        self.engs = {"pe": nc.tensor, "act": nc.scalar, "dve": nc.vector, "pool": nc.gpsimd, "sp": nc.sync}
        self.sems = {}
        self.cnt = {}
        for e in self.engs:
            self.sems[e] = es.enter_context(nc.semaphore("c_" + e))
            self.cnt[e] = 0
        self.waited = {e: {} for e in self.engs}
        self.res = {}
        self.n_inst = 0
        self.n_wait = 0

    def sb(self, name, shape, dt):
        return self.es.enter_context(self.nc.sbuf_tensor("sb_" + name, list(shape), dt))

    def ps(self, name, shape, dt=F32):
        return self.es.enter_context(self.nc.psum_tensor("ps_" + name, list(shape), dt))

    def R(self, key):
        r = self.res.get(key)
        if r is None:
            r = self.res[key] = Res(key)
        return r

    def lane(self, name):
        if name not in self.sems:
            self.sems[name] = self.es.enter_context(self.nc.semaphore("l_" + name))
            self.cnt[name] = 0
        return name

    def _wait(self, eng, tok, same_engine_ok=False):
        if tok is None:
            return
        key, val = tok
        if key == eng and same_engine_ok:
            return
        if self.waited[eng].get(key, 0) >= val:
            return
        self.engs[eng].wait_ge(self.sems[key], val)
        self.waited[eng][key] = val
        self.n_wait += 1

    def _deps(self, eng, reads, writes, pe_accum=False):
        for k in reads:
            r = self.R(k)
            self._wait(eng, r.w)
        for k in writes:
            r = self.R(k)
            self._wait(eng, r.w, same_engine_ok=True)
            for sk, v in r.r.items():
                self._wait(eng, (sk, v), same_engine_ok=True)

    def _commit(self, tok, reads, writes):
        for k in reads:
            r = self.R(k)
            if r.r.get(tok[0], 0) < tok[1]:
                r.r[tok[0]] = tok[1]
        for k in writes:
            r = self.R(k)
            r.w = tok
            r.r = {}

    def op(self, eng, fn, reads=(), writes=()):
        self._deps(eng, reads, writes)
        inst = fn()
        self.cnt[eng] += 1
        inst.then_inc(self.sems[eng], 1)
        tok = (eng, self.cnt[eng])
        self._commit(tok, reads, writes)
        self.n_inst += 1
        return tok

    def dma(self, q, lane, out, in_, reads=(), writes=(), **kw):
        self.lane(lane)
        self._deps(q, reads, writes)
        prev = self.cnt[lane]
        if prev:
            self._wait(q, (lane, prev))
        inst = self.engs[q].dma_start(out=out, in_=in_, **kw)
        self.cnt[lane] += 16
        inst.then_inc(self.sems[lane], 16)
        tok = (lane, self.cnt[lane])
        self._commit(tok, reads, writes)
        self.n_inst += 1
        return tok

    def wait_all(self, eng, keys):
        for k in keys:
            r = self.R(k)
            self._wait(eng, r.w)
            for sk, v in r.r.items():
                self._wait(eng, (sk, v))


class WStream:
    def __init__(self, kb, plan, nslots=8, kcmax=16):
        self.kb = kb
        self.plan = plan
        self.ns = nslots
        self.slots = [kb.sb(f"wslot{i}", [128, kcmax, 128], BF16) for i in range(nslots)]
        self.issued = 0
        self.used = 0
        for _ in range(min(nslots, len(plan))):
            self._issue()

    def _issue(self):
        i = self.issued
        ap, kc = self.plan[i]
        s = i % self.ns
        self.kb.dma("pool", f"w{s}", self.slots[s][:, 0:kc, :], ap, writes=[("wslot", s)])
        self.issued += 1

    def next(self, expect=None):
        i = self.used
        ap, kc = self.plan[i]
        if expect is not None:
            assert expect is ap or expect == ap, (i, expect, ap)
        self.used += 1
        s = i % self.ns
        return self.slots[s], ("wslot", s), i

    def done(self, i):
        if self.issued < len(self.plan) and self.issued == i + self.ns:
            self._issue()


def fm_vec(v):
    v = np.asarray(v, np.float32)
    return np.ascontiguousarray(v.reshape(-1, 128).T)


def prep_w(W):
    K, N = W.shape
    return np.ascontiguousarray(W.reshape(K // 128, 128, N // 128, 128).transpose(2, 1, 0, 3))


V_C = 0
V_BMOD = 32
V_G1 = 176
V_GMIX = 192
V_G2 = 208
V_GF = 224
V_QN = 240
V_KN = 241
NV = 242
CTX = 256
GN_EPS = 64e-5
CB = 128
DECAY_C = -0.6065306597126334
HV_MU = 0
HV_KK = 48
HV_KA = 64
HV_A0 = 80
HV_RK = 112
HV_LW = 128
HV_LB = 144
NHV = 160
NQH = 16
NKV = 4
ATT_SCALE = 128 ** -0.5


def rope_tables(T):
    rows = T // 64
    row = np.repeat(np.arange(rows, dtype=np.float32), 64)
    col = np.tile(np.arange(64, dtype=np.float32), rows)
    inv = (np.float32(10000.0) ** (-np.arange(0, 64, 2, dtype=np.float32) / np.float32(64))).astype(np.float32)
    ar = (row[:, None] * inv[None]).astype(np.float32)
    ac = (col[:, None] * inv[None]).astype(np.float32)
    cos = np.concatenate([np.cos(ar), np.cos(ar), np.cos(ac), np.cos(ac)], 1).T
    sin = np.concatenate([-np.sin(ar), np.sin(ar), -np.sin(ac), np.sin(ac)], 1).T
    perm = np.zeros((128, 128), np.float32)
    for d in range(128):
        partner = d + 32 if (d % 64) < 32 else d - 32
        perm[partner, d] = 1.0
    return np.ascontiguousarray(cos, np.float32), np.ascontiguousarray(sin, np.float32), perm


def rwkv_consts():
    ii = np.arange(CB)
    tri = np.zeros((4, CB, CB), np.float32)
    tri[0] = (ii[:, None] <= ii[None, :]) * DECAY_C
    tri[1] = (ii[:, None] < ii[None, :]) * DECAY_C
    tri[2] = (ii[:, None] >= ii[None, :]) * DECAY_C
    tri[3] = (ii[:, None] > ii[None, :]) * DECAY_C
    mA = np.zeros((2, CB, 4 * CB), np.float32)
    mB = np.zeros((2, CB, CB), np.float32)
    for z in range(2):
        before = (ii[:, None] < ii[None, :]) if z == 0 else (ii[:, None] > ii[None, :])
        beq = before | np.eye(CB, dtype=bool)
        mA[z] = np.concatenate([before, before.T, before, beq], 1)
        mB[z] = beq
    mq = np.zeros((2, 8, CB, CB), np.float32)
    for lv in range(7):
        blk = ii // (2 ** lv)
        q0 = (blk[:, None] % 2 == 1) & (blk[None, :] == blk[:, None] - 1)
        mq[0, lv] = q0
        mq[1, lv] = q0.T
    mq[0, 7] = mq[0, 0].T
    mq[1, 7] = mq[1, 0].T
    return (np.ascontiguousarray(tri.transpose(1, 0, 2)), np.ascontiguousarray(mA.transpose(1, 0, 2)),
            np.ascontiguousarray(mB.transpose(1, 0, 2)), np.eye(128, dtype=np.float32),
            np.ascontiguousarray(mq.transpose(2, 0, 1, 3)))


def head_fm(v):
    return np.ascontiguousarray(np.asarray(v, np.float32).reshape(16, 64).T)


def build_program(cfg):
    T = cfg["T"]
    TT = cfg.get("TT", 512)
    stages = cfg.get("stages", "full")
    rw_on = stages == "full"
    NT = T // TT
    NK = T + CTX
    NKC = NK // 128
    NP = CTX + T + 4
    COL_CTX = 1
    COL_LAT = CTX + 3
    nc = bass.Bass("TRN2", target_bir_lowering=False)
    dt = lambda name, shape, dtype=F32, kind="ExternalInput": nc.dram_tensor(name, list(shape), dtype, kind=kind).ap()
    xT = dt("xT", [D, T])
    cxT = dt("cxT", [D, CTX])
    vecs = dt("vecs", [128, NV])
    wmod = dt("wmod", [NMOD * NCH, 128, NCH, 128])
    w1i = dt("w1i", [2 * NJ, 128, NCH, 128])
    w1o = dt("w1o", [NCH, 128, NJ, 128])
    outT = dt("outT", [D, T], kind="ExternalOutput")
    w2i = dt("w2i", [2 * NJ, 128, NCH, 128])
    w2o = dt("w2o", [NCH, 128, NJ, 128])
    wq = dt("wq", [16, 128, NCH, 128])
    wk = dt("wk", [4, 128, NCH, 128])
    wv = dt("wv", [4, 128, NCH, 128])
    wga = dt("wga", [16, 128, NCH, 128])
    woa = dt("woa", [16, 128, NCH, 128])
    wout = dt("wout", [16, 128, NCH, 128])
    cosT = dt("cosT", [128, T])
    sinT = dt("sinT", [128, T])
    perm = dt("perm", [128, 128])
    x1T = dt("x1T", [D, T], F32, "Internal")
    qS = dt("qS", [NQH, 128, T], BF16, "Internal")
    kS = dt("kS", [NKV, 128, NK], BF16, "Internal")
    vS = dt("vS", [NKV, 128, NKC, 128], BF16, "Internal")
    gaS = dt("gaS", [NCH, 128, T], BF16, "Internal")
    if rw_on:
        wrw = dt("wrw", [27, 128, NCH, 128])
        wgr = dt("wgr", [16, 128, NCH, 128])
        wor = dt("wor", [16, 128, 8, 128])
        w2p_d = dt("w2p", [128, 2, 1024])
        a2p_d = dt("a2p", [128, 2, 1024])
        g2_d = dt("g2", [128, 1024])
        w0r_d = dt("w0r", [1, 2, 1024])
        hvec_d = dt("hvec", [64, NHV])
        svec_d = dt("svec", [128, 3])
        tri_d = dt("tri", [128, 4, 128])
        mkA_d = dt("mkA", [128, 2, 512])
        mkB_d = dt("mkB", [128, 2, 128])
        id_d = dt("ident", [128, 128])
        mq_d = dt("mq", [128, 2, 8, 128])
        grS = dt("grS", [NCH, 128, T], BF16, "Internal")
        dk_ = "ExternalOutput" if cfg.get("debug") else "Internal"
        prS = dt("prS", [27, 128, NP], F32, dk_)
        yS = dt("yS", [16, 64, T], F32, dk_)
        bS = dt("bS", [16, 64, T], F32, dk_)
        rwS = dt("rwS", [8, 128, T], BF16, dk_)
        if cfg.get("debug"):
            dbgT = dt("dbgT", [2, 12, 128, 512], F32, "ExternalOutput")

    xT_v = xT.rearrange("(c p) t -> p c t", p=128)
    cxT_v = cxT.rearrange("(c p) t -> p c t", p=128)
    outT_v = outT.rearrange("(c p) t -> p c t", p=128)

    plan = []
    for q in range(NMOD * NCH):
        plan.append((wmod[q], NCH))

    def ffn_plan(wi, wo):
        for j in range(NJ):
            plan.append((wi[j], NCH))
            plan.append((wi[NJ + j], NCH))
        for oc in range(NCH):
            for (k0, k1) in ((0, 16), (16, 32), (32, 44)):
                plan.append((wo[oc][:, k0:k1, :], k1 - k0))

    def lin_plan(w, n, kc=NCH):
        for oc in range(n):
            plan.append((w[oc], kc))

    tiles1 = [("ctx", 0, CTX)] + [("lat", it * TT, TT) for it in range(NT)]
    for kind, t0, n in tiles1:
        ffn_plan(w1i, w1o)
        if kind == "lat":
            lin_plan(wq, 16)
        lin_plan(wk, 4)
        lin_plan(wv, 4)
        if kind == "lat":
            lin_plan(wga, 16)
        if rw_on:
            lin_plan(wrw, 27)
            if kind == "lat":
                lin_plan(wgr, 16)
    for it in range(NT):
        lin_plan(woa, 16)
        if rw_on:
            lin_plan(wor, 16, 8)
        lin_plan(wout, 16)
        ffn_plan(w2i, w2o)

    with contextlib.ExitStack() as es:
        kb = KB(nc, es)
        ws = WStream(kb, plan, nslots=6)
        uid = [0]

        def sbs(stack, name, shape, dtype=F32):
            uid[0] += 1
            return stack.enter_context(nc.sbuf_tensor(f"s{uid[0]}_{name}", list(shape), dtype))

        def barrier():
            keys = list(kb.cnt.keys())
            for e in kb.engs:
                for k in keys:
                    if k != e and kb.cnt[k] > 0:
                        kb._wait(e, (k, kb.cnt[k]))

        vt = kb.sb("vecs", [128, NV], F32)
        kb.dma("sp", "misc", vt[:], vecs[:, :], writes=["vecs"])
        ones_d = kb.sb("ones_d", [128, 128], BF16)
        kb.op("dve", lambda: nc.vector.memset(ones_d[:], 1.0 / D), writes=["ones"])
        ones_h = kb.sb("ones_h", [128, 128], BF16)
        kb.op("dve", lambda: nc.vector.memset(ones_h[:], 1.0 / 128), writes=["ones"])
        ones_1 = kb.sb("ones_1", [128, 128], BF16)
        kb.op("dve", lambda: nc.vector.memset(ones_1[:], 1.0), writes=["ones"])
        silu_c = kb.sb("silu_c", [128, NCH * 2], BF16)
        kb.op("act", lambda: nc.scalar.activation(out=silu_c[:], in_=vt[:, V_C:V_C + 32], func=AF.Silu),
              reads=["vecs"], writes=["silu_c"])
        modv = kb.sb("modv", [128, NMOD * NCH, 2], F32)
        banks = [kb.ps(f"bank{i}", [128, 512], F32) for i in range(8)]
        bank_i = [0]
        NROT = [6]

        def bank():
            i = bank_i[0] % NROT[0]
            bank_i[0] += 1
            return banks[i], ("bank", i)

        perm_b = kb.sb("perm_b", [128, 128], BF16)
        kb.dma("pool", "misc2", perm_b[:], perm[:, :], writes=["perm_b"])
        eps_t = kb.sb("eps_t", [128, 1], F32)
        kb.op("dve", lambda: nc.vector.memset(eps_t[:], EPS), writes=["eps_t"])

        for q in range(NMOD * NCH):
            w, wk_, wi_ = ws.next()
            pb, pk = bank()
            for kc in range(NCH):
                kb.op("pe", lambda kc=kc: nc.tensor.matmul(pb[:, 0:2], lhsT=w[:, kc, :], rhs=silu_c[:, 2 * kc:2 * kc + 2],
                                                          start=(kc == 0), stop=(kc == NCH - 1)),
                      reads=[wk_, "silu_c"], writes=[pk])
            ws.done(wi_)
            kb.op("dve", lambda q=q: nc.vector.tensor_scalar(out=modv[:, q, :], in0=pb[:, 0:2],
                                                            scalar1=vt[:, V_BMOD + q:V_BMOD + q + 1], scalar2=None,
                                                            op0=ALU.add),
                  reads=[pk, "vecs"], writes=["modv"])

        mA = kb.sb("mA", [128, 3, NCH, 2], F32)
        mS = kb.sb("mS", [128, 3, NCH, 2], F32)
        mG = kb.sb("mG", [128, 3, NCH, 2], F32)
        for sl, gcol in ((0, V_G1), (1, V_GMIX), (2, V_G2)):
            for wh in range(2):
                sh = modv[:, (3 * sl) * NCH:(3 * sl + 1) * NCH, wh]
                sc = modv[:, (3 * sl + 1) * NCH:(3 * sl + 2) * NCH, wh]
                gt = modv[:, (3 * sl + 2) * NCH:(3 * sl + 3) * NCH, wh]
                kb.op("dve", lambda sl=sl, wh=wh, sc=sc, gcol=gcol: nc.vector.scalar_tensor_tensor(
                    out=mA[:, sl, :, wh], in0=sc, scalar=1.0, in1=vt[:, gcol:gcol + NCH], op0=ALU.add, op1=ALU.mult),
                    reads=["modv", "vecs"], writes=["mA"])
                kb.op("dve", lambda sl=sl, wh=wh, sh=sh: nc.vector.tensor_copy(out=mS[:, sl, :, wh], in_=sh),
                      reads=["modv"], writes=["mS"])
                kb.op("dve", lambda sl=sl, wh=wh, gt=gt: nc.vector.tensor_scalar(
                    out=mG[:, sl, :, wh], in0=gt, scalar1=(0.5 if sl != 1 else 1.0), scalar2=None, op0=ALU.mult),
                    reads=["modv"], writes=["mG"])

        Bf = {}
        tmp_i = [0]

        def alloc_tile_bufs(stack):
            Bf["xt0"] = sbs(stack, "xt0", [128, NCH, TT], F32)
            Bf["sq"] = sbs(stack, "sq", [128, NCH, TT], BF16)
            Bf["hb"] = sbs(stack, "hb", [128, NCH, TT], BF16)
            Bf["tmps"] = [sbs(stack, f"tmp{i}", [128, TT], F32) for i in range(4)]
            Bf["rstd"] = sbs(stack, "rstd", [128, TT], F32)
            Bf["act"] = sbs(stack, "act", [128, NJ, TT], BF16)

        def tmpbuf():
            i = tmp_i[0] % 4
            tmp_i[0] += 1
            return Bf["tmps"][i], ("tmp", i)

        def rms_rstd(src_fn, src_keys, n, ones):
            rstd = Bf["rstd"]
            pb, pk = bank()
            m = len(src_keys)
            for c in range(m):
                kb.op("pe", lambda c=c: nc.tensor.matmul(pb[:, 0:n], lhsT=ones[:], rhs=src_fn(c),
                                                        start=(c == 0), stop=(c == m - 1)),
                      reads=["ones", src_keys[c]], writes=[pk])
            kb.op("act", lambda: nc.scalar.activation(out=rstd[:, 0:n], in_=pb[:, 0:n], func=AF.Sqrt, bias=eps_t[:], scale=1.0),
                  reads=[pk, "eps_t"], writes=["rstd"])
            kb.op("dve", lambda: nc.vector.reciprocal(out=rstd[:, 0:n], in_=rstd[:, 0:n]), reads=["rstd"], writes=["rstd"])

        def rms_modulate(x, xk, sl, wh, n):
            sq, hb, rstd = Bf["sq"], Bf["hb"], Bf["rstd"]
            kb.op("act", lambda: nc.scalar.activation(out=sq[:, :, 0:n], in_=x[:, :, 0:n], func=AF.Square),
                  reads=[xk], writes=["sq"])
            rms_rstd(lambda c: sq[:, c, 0:n], ["sq"] * NCH, n, ones_d)
            for c in range(NCH):
                tb, tk = tmpbuf()
                kb.op("dve", lambda c=c, tb=tb: nc.vector.tensor_tensor(out=tb[:, 0:n], in0=x[:, c, 0:n], in1=rstd[:, 0:n], op=ALU.mult),
                      reads=[xk, "rstd"], writes=[tk])
                kb.op("dve", lambda c=c, tb=tb: nc.vector.tensor_scalar(out=hb[:, c, 0:n], in0=tb[:, 0:n],
                                                                       scalar1=mA[:, sl, c, wh:wh + 1], scalar2=mS[:, sl, c, wh:wh + 1],
                                                                       op0=ALU.mult, op1=ALU.add),
                      reads=[tk, "mA", "mS"], writes=["hb"])

        def linear(src, src_key_fn, n, nchunks, epilogue, kcn=NCH):
            for oc in range(nchunks):
                w, wkey, wi_ = ws.next()
                pb, pk = bank()
                for kc in range(kcn):
                    kb.op("pe", lambda kc=kc: nc.tensor.matmul(pb[:, 0:n], lhsT=w[:, kc, :], rhs=src[:, kc, 0:n],
                                                              start=(kc == 0), stop=(kc == kcn - 1)),
                          reads=[wkey, src_key_fn(kc)], writes=[pk])
                ws.done(wi_)
                epilogue(oc, pb, pk)

        def ffn(x, xk, sl, wh, n):
            hb, act = Bf["hb"], Bf["act"]
            rms_modulate(x, xk, sl, wh, n)
            for j in range(NJ):
                wg, wgk, ig = ws.next()
                wu, wuk, iu = ws.next()
                pg, pgk = bank()
                pu, puk = bank()
                for kc in range(NCH):
                    kb.op("pe", lambda kc=kc: nc.tensor.matmul(pg[:, 0:n], lhsT=wg[:, kc, :], rhs=hb[:, kc, 0:n],
                                                              start=(kc == 0), stop=(kc == NCH - 1)),
                          reads=[wgk, "hb"], writes=[pgk])
                ws.done(ig)
                for kc in range(NCH):
                    kb.op("pe", lambda kc=kc: nc.tensor.matmul(pu[:, 0:n], lhsT=wu[:, kc, :], rhs=hb[:, kc, 0:n],
                                                              start=(kc == 0), stop=(kc == NCH - 1)),
                          reads=[wuk, "hb"], writes=[puk])
                ws.done(iu)
                s, sk = tmpbuf()
                kb.op("act", lambda s=s: nc.scalar.activation(out=s[:, 0:n], in_=pg[:, 0:n], func=AF.Silu),
                      reads=[pgk], writes=[sk])
                kb.op("dve", lambda s=s, j=j: nc.vector.tensor_tensor(out=act[:, j, 0:n], in0=s[:, 0:n], in1=pu[:, 0:n], op=ALU.mult),
                      reads=[sk, puk], writes=[("act", j)])
            for oc in range(NCH):
                po, pok = bank()
                for (k0, k1) in ((0, 16), (16, 32), (32, 44)):
                    w, wkey, wi_ = ws.next()
                    for kc in range(k0, k1):
                        kb.op("pe", lambda kc=kc, k0=k0: nc.tensor.matmul(po[:, 0:n], lhsT=w[:, kc - k0, :], rhs=act[:, kc, 0:n],
                                                                         start=(kc == 0), stop=(kc == NJ - 1)),
                              reads=[wkey, ("act", kc)], writes=[pok])
                    ws.done(wi_)
                kb.op("dve", lambda oc=oc: nc.vector.scalar_tensor_tensor(
                    out=x[:, oc, 0:n], in0=po[:, 0:n], scalar=mG[:, sl, oc, wh:wh + 1], in1=x[:, oc, 0:n],
                    op0=ALU.mult, op1=ALU.add),
                    reads=[pok, "mG", xk], writes=[xk])

        def final_norm_store(x, xk, t0, n):
            sq, rstd = Bf["sq"], Bf["rstd"]
            kb.op("act", lambda: nc.scalar.activation(out=sq[:, :, 0:n], in_=x[:, :, 0:n], func=AF.Square),
                  reads=[xk], writes=["sq"])
            rms_rstd(lambda c: sq[:, c, 0:n], ["sq"] * NCH, n, ones_d)
            for c in range(NCH):
                kb.op("dve", lambda c=c: nc.vector.scalar_tensor_tensor(
                    out=x[:, c, 0:n], in0=x[:, c, 0:n], scalar=vt[:, V_GF + c:V_GF + c + 1], in1=rstd[:, 0:n],
                    op0=ALU.mult, op1=ALU.mult),
                    reads=[xk, "rstd", "vecs"], writes=[xk])
            kb.dma("sp", "st", outT_v[:, :, t0:t0 + n], x[:, :, 0:n], reads=[xk])

        xk = "xt0"
        qS_v = qS.rearrange("h p t -> p h t")
        kS_v = kS.rearrange("h p t -> p h t")
        gaS_v = gaS.rearrange("h p t -> p h t")
        x1T_v = x1T.rearrange("(c p) t -> p c t", p=128)
        if rw_on:
            grS_v = grS.rearrange("h p t -> p h t")
            prS_v = prS.rearrange("c p t -> p c t")
            rwS_v = rwS.rearrange("c p t -> p c t")

        with contextlib.ExitStack() as p1:
            alloc_tile_bufs(p1)
            x, sq, hb, act, rstd = Bf["xt0"], Bf["sq"], Bf["hb"], Bf["act"], Bf["rstd"]
            cs = sbs(p1, "cs", [128, TT], F32)
            sn = sbs(p1, "sn", [128, TT], F32)
            qnb = sbs(p1, "qnb", [128, TT], BF16)
            qsq = sbs(p1, "qsq", [128, TT], BF16)
            vst = sbs(p1, "vst", [128, 512], BF16)
            if rw_on:
                zt = sbs(p1, "zt", [128, 27], F32)
                kb.op("dve", lambda: nc.vector.memset(zt[:], 0.0), writes=["zt"])
                for col in (0, CTX + 1, CTX + 2, CTX + 3 + T):
                    kb.dma("sp", "stz", prS_v[:, :, col:col + 1], zt[:, :].rearrange("p (c o) -> p c o", o=1),
                           reads=["zt"], writes=["hbm_pr"], allow_slow_non_contiguous=True)
            for kind, t0, n in tiles1:
                wh = 1 if kind == "ctx" else 0
                src = cxT_v if kind == "ctx" else xT_v
                kb.dma("sp", "ldx", x[:, :, 0:n], src[:, :, t0:t0 + n], writes=[xk])
                ffn(x, xk, 0, wh, n)
                if kind == "lat":
                    kb.dma("sp", "stx1", x1T_v[:, :, t0:t0 + n], x[:, :, 0:n], reads=[xk], writes=["hbm_x1"])
                    kb.dma("sp", "ldcs", cs[:, 0:n], cosT[:, t0:t0 + n], writes=["cs"])
                    kb.dma("sp", "ldsn", sn[:, 0:n], sinT[:, t0:t0 + n], writes=["sn"])
                rms_modulate(x, xk, 1, wh, n)

                def qk_epi(gcol, rope, dst_j0):
                    def epi(oc, pb, pk):
                        kb.op("act", lambda: nc.scalar.activation(out=qsq[:, 0:n], in_=pb[:, 0:n], func=AF.Square),
                              reads=[pk], writes=["qsq"])
                        rms_rstd(lambda c: qsq[:, 0:n], ["qsq"], n, ones_h)
                        dst = act[:, dst_j0 + oc, 0:n]
                        dk = ("act", dst_j0 + oc)
                        if not rope:
                            kb.op("dve", lambda: nc.vector.scalar_tensor_tensor(
                                out=dst, in0=pb[:, 0:n], scalar=vt[:, gcol:gcol + 1], in1=rstd[:, 0:n],
                                op0=ALU.mult, op1=ALU.mult), reads=[pk, "vecs", "rstd"], writes=[dk])
                            return
                        kb.op("dve", lambda: nc.vector.scalar_tensor_tensor(
                            out=qnb[:, 0:n], in0=pb[:, 0:n], scalar=vt[:, gcol:gcol + 1], in1=rstd[:, 0:n],
                            op0=ALU.mult, op1=ALU.mult), reads=[pk, "vecs", "rstd"], writes=["qnb"])
                        pp, ppk = bank()
                        kb.op("pe", lambda: nc.tensor.matmul(pp[:, 0:n], lhsT=perm_b[:], rhs=qnb[:, 0:n], start=True, stop=True),
                              reads=["perm_b", "qnb"], writes=[ppk])
                        t1, t1k = tmpbuf()
                        t2, t2k = tmpbuf()
                        kb.op("dve", lambda: nc.vector.tensor_tensor(out=t1[:, 0:n], in0=qnb[:, 0:n], in1=cs[:, 0:n], op=ALU.mult),
                              reads=["qnb", "cs"], writes=[t1k])
                        kb.op("dve", lambda: nc.vector.tensor_tensor(out=t2[:, 0:n], in0=pp[:, 0:n], in1=sn[:, 0:n], op=ALU.mult),
                              reads=[ppk, "sn"], writes=[t2k])
                        kb.op("dve", lambda: nc.vector.tensor_tensor(out=dst, in0=t1[:, 0:n], in1=t2[:, 0:n], op=ALU.add),
                              reads=[t1k, t2k], writes=[dk])
                    return epi

                hk = lambda kc: "hb"
                if kind == "lat":
                    linear(hb, hk, n, 16, qk_epi(V_QN, True, 0))
                    kb.dma("sp", "stq", qS_v[:, :, t0:t0 + n], act[:, 0:16, 0:n],
                           reads=[("act", j) for j in range(16)], writes=["hbm_q"])
                linear(hb, hk, n, 4, qk_epi(V_KN, kind == "lat", 16))
                koff = t0 if kind == "lat" else T
                kb.dma("sp", "stk", kS_v[:, :, koff:koff + n], act[:, 16:20, 0:n],
                       reads=[("act", j) for j in range(16, 20)], writes=["hbm_k"])
                wvs = [ws.next() for _ in range(4)]
                for tb_ in range(n // 128):
                    pb, pk = bank()
                    for g in range(4):
                        w, wkey, _ = wvs[g]
                        for kc in range(NCH):
                            kb.op("pe", lambda kc=kc, g=g, w=w: nc.tensor.matmul(
                                pb[:, g * 128:(g + 1) * 128], lhsT=hb[:, kc, tb_ * 128:(tb_ + 1) * 128], rhs=w[:, kc, :],
                                start=(kc == 0), stop=(kc == NCH - 1)),
                                reads=[wkey, "hb"], writes=[pk])
                    kb.op("act", lambda: nc.scalar.copy(out=vst[:, :], in_=pb[:, :]), reads=[pk], writes=["vst"])
                    cblk = (koff + tb_ * 128) // 128
                    kb.dma("sp", "stv", vS[:, :, cblk, :].rearrange("g p e -> p g e"),
                           vst[:, :].rearrange("p (g e) -> p g e", g=4), reads=["vst"], writes=["hbm_v"])
                for g in range(4):
                    ws.done(wvs[g][2])
                if kind == "lat":
                    def ga_epi(oc, pb, pk):
                        kb.op("act", lambda: nc.scalar.activation(out=act[:, 20 + oc, 0:n], in_=pb[:, 0:n], func=AF.Sigmoid),
                              reads=[pk], writes=[("act", 20 + oc)])
                    linear(hb, hk, n, 16, ga_epi)
                    kb.dma("sp", "stga", gaS_v[:, :, t0:t0 + n], act[:, 20:36, 0:n],
                           reads=[("act", j) for j in range(20, 36)], writes=["hbm_ga"])
                if rw_on:
                    pcol = (COL_LAT if kind == "lat" else COL_CTX) + t0

                    def rw_epi(oc, pb, pk):
                        tb, tk = tmpbuf()
                        kb.op("act", lambda: nc.scalar.copy(out=tb[:, 0:n], in_=pb[:, 0:n]), reads=[pk], writes=[tk])
                        kb.dma("sp", f"stp{tk[1]}", prS[oc][:, pcol:pcol + n], tb[:, 0:n], reads=[tk], writes=["hbm_pr"])
                    linear(hb, hk, n, 27, rw_epi)
                    if kind == "lat":
                        def gr_epi(oc, pb, pk):
                            kb.op("act", lambda: nc.scalar.activation(out=act[:, oc, 0:n], in_=pb[:, 0:n], func=AF.Sigmoid),
                                  reads=[pk], writes=[("act", oc)])
                        linear(hb, hk, n, 16, gr_epi)
                        kb.dma("sp", "stgr", grS_v[:, :, t0:t0 + n], act[:, 0:16, 0:n],
                               reads=[("act", j) for j in range(16)], writes=["hbm_gr"])
            barrier()

        if rw_on:
            with contextlib.ExitStack() as pr:
                NROT[0] = 8
                R_ = lambda name, shape, dtype=F32: sbs(pr, name, shape, dtype)
                w2p = R_("w2p", [128, 2, 1024]); a2p = R_("a2p", [128, 2, 1024]); g2t = R_("g2t", [128, 1024])
                w0r = R_("w0r", [1, 2, 1024]); onesr = R_("onesr", [1, 128])
                tri = R_("tri", [128, 4, 128]); mkA = R_("mkA", [128, 2, 512]); mkB = R_("mkB", [128, 2, 128])
                mq = R_("mq", [128, 2, 8, 128])
                kb.dma("sp", "ldc_mq", mq[:, :, :, :], mq_d[:, :, :, :], writes=["mq"])
                ident = R_("ident", [128, 128]); o64 = R_("o64", [64, 64]); o64m = R_("o64m", [64, 64])
                hv = R_("hv", [64, NHV]); hv1 = R_("hv1", [64, 64]); hvh = R_("hvh", [64, 48]); sv = R_("sv", [128, 3])
                sv1 = R_("sv1", [128, 3]); svh = R_("svh", [128, 3]); gne = R_("gne", [64, 1])
                Hs = R_("Hs", [64, 16, 64])
                for tl_, dr_, nm in ((w2p, w2p_d, "w2p"), (a2p, a2p_d, "a2p"), (tri, tri_d, "tri"), (mkA, mkA_d, "mkA"),
                                     (mkB, mkB_d, "mkB"), (w0r, w0r_d, "w0r")):
                    kb.dma("sp", "ldc_" + nm, tl_[:, :, :], dr_[:, :, :], writes=[nm])
                for tl_, dr_, nm in ((g2t, g2_d, "g2t"), (ident, id_d, "ident"), (hv, hvec_d, "hv"), (sv, svec_d, "sv")):
                    kb.dma("sp", "ldc_" + nm, tl_[:, :], dr_[:, :], writes=[nm])
                kb.op("dve", lambda: nc.vector.memset(onesr[:], 1.0), writes=["onesr"])
                kb.op("dve", lambda: nc.vector.memset(o64[:], 1.0), writes=["o64"])
                kb.op("dve", lambda: nc.vector.memset(o64m[:], 1.0 / 64), writes=["o64m"])
                kb.op("dve", lambda: nc.vector.memset(gne[:], GN_EPS), writes=["gne"])
                kb.op("dve", lambda: nc.vector.tensor_scalar(out=hv1[:, 0:48], in0=hv[:, HV_MU:HV_MU + 48], scalar1=-1.0, scalar2=1.0,
                                                            op0=ALU.mult, op1=ALU.add), reads=["hv"], writes=["hv1"])
                kb.op("dve", lambda: nc.vector.tensor_scalar(out=hv1[:, 48:64], in0=hv[:, HV_KA:HV_KA + 16], scalar1=-1.0, scalar2=1.0,
                                                            op0=ALU.mult, op1=ALU.add), reads=["hv"], writes=["hv1"])
                kb.op("dve", lambda: nc.vector.tensor_scalar(out=hvh[:, :], in0=hv[:, HV_MU:HV_MU + 48], scalar1=0.5, scalar2=None,
                                                            op0=ALU.mult), reads=["hv"], writes=["hvh"])
                kb.op("dve", lambda: nc.vector.tensor_scalar(out=sv1[:, :], in0=sv[:, :], scalar1=-1.0, scalar2=1.0,
                                                            op0=ALU.mult, op1=ALU.add), reads=["sv"], writes=["sv1"])
                kb.op("dve", lambda: nc.vector.tensor_scalar(out=svh[:, :], in0=sv[:, :], scalar1=0.5, scalar2=None,
                                                            op0=ALU.mult), reads=["sv"], writes=["svh"])
                shr = R_("shr", [128, 3, CB + 2]); shs = R_("shs", [128, 3, CB]); shq = R_("shq", [128, 3, CB])
                psh = R_("psh", [128, 3, CB]); tlw = R_("tlw", [128, CB]); sgw = R_("sgw", [128, 1024]); sgl = R_("sgl", [128, CB])
                NSET = cfg.get("rw_group", 4)
                S = []
                for i in range(NSET):
                    d = {}
                    d["raw"] = R_(f"raw{i}", [64, 3, CB + 2])
                    for nm in ("s3", "q3", "pm3"):
                        d[nm] = R_(f"{nm}{i}", [64, 3, CB])
                    for nm in ("Ep", "Ex", "Em", "az", "kk", "t1", "kkn", "kmod", "bb", "rt", "at", "bt", "kt", "bh", "kh", "bon",
                               "WT", "y0", "b0", "yv", "yc", "t2"):
                        d[nm] = R_(f"{nm}{i}", [64, CB])
                    d["TM"] = R_(f"TM{i}", [128, 192])
                    d["SA"] = R_(f"SA{i}", [128, 512])
                    d["SB"] = R_(f"SB{i}", [128, 128])
                    d["X"] = R_(f"X{i}", [128, 128])
                    for nm in ("Tm", "TT", "P1s", "tq"):
                        d[nm] = R_(f"{nm}{i}", [128, 128])
                    d["U0"] = R_(f"U0{i}", [128, 64])
                    d["U"] = R_(f"U{i}", [128, 64])
                    d["ob"] = R_(f"ob{i}", [64, CB], BF16)
                    S.append(d)
                prS_h = prS[0:24].rearrange("(g c) p t -> p g c t", g=3)
                rwS_h = rwS.rearrange("c (s p) t -> (c s) p t", s=2)
                unit = [0]

                def head_block(z, h, c0, tok0, emit, i):
                    d = S[i]
                    K_ = lambda nm: (nm, i)
                    hc, hs_ = h // 2, h % 2
                    colv = lambda base: hv[:, base + h:base + h + 1]
                    kb.dma("sp", f"ldraw{i}", d["raw"][:, :, :], prS_h[hs_ * 64:(hs_ + 1) * 64, :, hc, c0 - 1:c0 + CB + 1],
                           reads=["hbm_pr"], writes=[K_("raw")])
                    kb.op("dve", lambda: nc.vector.tensor_tensor(out=d["s3"][:, :, :], in0=d["raw"][:, :, 0:CB], in1=d["raw"][:, :, 2:CB + 2], op=ALU.add),
                          reads=[K_("raw")], writes=[K_("s3")])
                    for g in range(3):
                        kb.op("dve", lambda g=g: nc.vector.tensor_scalar(out=d["q3"][:, g, :], in0=d["raw"][:, g, 1:CB + 1],
                                                                        scalar1=hv1[:, g * 16 + h:g * 16 + h + 1], scalar2=None, op0=ALU.mult),
                              reads=[K_("raw"), "hv1"], writes=[K_("q3")])
                        kb.op("dve", lambda g=g: nc.vector.scalar_tensor_tensor(out=d["pm3"][:, g, :], in0=d["s3"][:, g, :],
                                                                               scalar=hvh[:, g * 16 + h:g * 16 + h + 1], in1=d["q3"][:, g, :],
                                                                               op0=ALU.mult, op1=ALU.add),
                              reads=[K_("s3"), K_("q3"), "hvh"], writes=[K_("pm3")])
                    rh, kh_, vh = d["pm3"][:, 0, :], d["pm3"][:, 1, :], d["pm3"][:, 2, :]
                    yield
                    pb, pk = bank()
                    kb.op("pe", lambda: nc.tensor.matmul(pb[0:64, 0:CB], lhsT=sgw[:, h * 64:(h + 1) * 64], rhs=tri[:, 2 * z, :], start=True, stop=True),
                          reads=["sgw", "tri"], writes=[pk])
                    kb.op("pe", lambda: nc.tensor.matmul(pb[0:64, CB:2 * CB], lhsT=sgw[:, h * 64:(h + 1) * 64], rhs=tri[:, 2 * z + 1, :], start=True, stop=True),
                          reads=["sgw", "tri"], writes=[pk])
                    kb.op("act", lambda: nc.scalar.activation(out=d["Ep"][:, :], in_=pb[0:64, 0:CB], func=AF.Exp), reads=[pk], writes=[K_("Ep")])
                    kb.op("act", lambda: nc.scalar.activation(out=d["Em"][:, :], in_=pb[0:64, 0:CB], func=AF.Exp, scale=-1.0), reads=[pk], writes=[K_("Em")])
                    kb.op("act", lambda: nc.scalar.activation(out=d["Ex"][:, :], in_=pb[0:64, CB:2 * CB], func=AF.Exp), reads=[pk], writes=[K_("Ex")])
                    pcl = CB - 1 if z == 0 else 0
                    yield
                    PC = d["Ep"][:, pcl:pcl + 1]
                    pa, pak = bank()
                    kb.op("pe", lambda: nc.tensor.matmul(pa[0:64, 0:CB], lhsT=a2p[:, z, h * 64:(h + 1) * 64], rhs=psh[:, 1, :], start=True, stop=True),
                          reads=["a2p", "psh"], writes=[pak])
                    kb.op("act", lambda: nc.scalar.activation(out=d["az"][:, :], in_=pa[0:64, 0:CB], func=AF.Sigmoid,
                                                             bias=hv[:, HV_A0 + z * 16 + h:HV_A0 + z * 16 + h + 1], scale=1.0),
                          reads=[pak, "hv"], writes=[K_("az")])
                    yield
                    kb.op("dve", lambda: nc.vector.tensor_scalar(out=d["kk"][:, :], in0=kh_, scalar1=colv(HV_KK), scalar2=None, op0=ALU.mult),
                          reads=[K_("pm3"), "hv"], writes=[K_("kk")])
                    kb.op("dve", lambda: nc.vector.tensor_tensor(out=d["t1"][:, :], in0=d["kk"][:, :], in1=d["kk"][:, :], op=ALU.mult),
                          reads=[K_("kk")], writes=[K_("t1")])
                    pn_, pnk = bank()
                    kb.op("pe", lambda: nc.tensor.matmul(pn_[0:64, 0:CB], lhsT=o64[:, :], rhs=d["t1"][:, :], start=True, stop=True),
                          reads=["o64", K_("t1")], writes=[pnk])
                    kb.op("act", lambda: nc.scalar.activation(out=d["t2"][:, :], in_=pn_[0:64, 0:CB], func=AF.Sqrt), reads=[pnk], writes=[K_("t2")])
                    kb.op("dve", lambda: nc.vector.tensor_scalar(out=d["t2"][:, :], in0=d["t2"][:, :], scalar1=1e-12, scalar2=None, op0=ALU.max),
                          reads=[K_("t2")], writes=[K_("t2")])
                    kb.op("dve", lambda: nc.vector.reciprocal(out=d["t2"][:, :], in_=d["t2"][:, :]), reads=[K_("t2")], writes=[K_("t2")])
                    kb.op("dve", lambda: nc.vector.tensor_tensor(out=d["kkn"][:, :], in0=d["kk"][:, :], in1=d["t2"][:, :], op=ALU.mult),
                          reads=[K_("kk"), K_("t2")], writes=[K_("kkn")])
                    yield
                    kb.op("dve", lambda: nc.vector.tensor_scalar(out=d["t1"][:, :], in0=d["az"][:, :], scalar1=colv(HV_KA), scalar2=hv1[:, 48 + h:49 + h],
                                                                op0=ALU.mult, op1=ALU.add), reads=[K_("az"), "hv", "hv1"], writes=[K_("t1")])
                    kb.op("dve", lambda: nc.vector.tensor_tensor(out=d["kmod"][:, :], in0=kh_, in1=d["t1"][:, :], op=ALU.mult),
                          reads=[K_("pm3"), K_("t1")], writes=[K_("kmod")])
                    kb.op("dve", lambda: nc.vector.tensor_tensor(out=d["bb"][:, :], in0=d["kkn"][:, :], in1=d["az"][:, :], op=ALU.mult),
                          reads=[K_("kkn"), K_("az")], writes=[K_("bb")])
                    yield
                    kb.op("dve", lambda: nc.vector.tensor_tensor(out=d["rt"][:, :], in0=rh, in1=d["Ep"][:, :], op=ALU.mult),
                          reads=[K_("pm3"), K_("Ep")], writes=[K_("rt")])
                    kb.op("dve", lambda: nc.vector.scalar_tensor_tensor(out=d["at"][:, :], in0=d["kkn"][:, :], scalar=-1.0, in1=d["Ex"][:, :],
                                                                       op0=ALU.mult, op1=ALU.mult), reads=[K_("kkn"), K_("Ex")], writes=[K_("at")])
                    kb.op("dve", lambda: nc.vector.tensor_tensor(out=d["bt"][:, :], in0=d["bb"][:, :], in1=d["Em"][:, :], op=ALU.mult),
                          reads=[K_("bb"), K_("Em")], writes=[K_("bt")])
                    kb.op("dve", lambda: nc.vector.tensor_tensor(out=d["kt"][:, :], in0=d["kmod"][:, :], in1=d["Em"][:, :], op=ALU.mult),
                          reads=[K_("kmod"), K_("Em")], writes=[K_("kt")])
                    kb.op("dve", lambda: nc.vector.tensor_scalar(out=d["bh"][:, :], in0=d["bt"][:, :], scalar1=PC, scalar2=None, op0=ALU.mult),
                          reads=[K_("bt"), K_("Ep")], writes=[K_("bh")])
                    kb.op("dve", lambda: nc.vector.tensor_scalar(out=d["kh"][:, :], in0=d["kt"][:, :], scalar1=PC, scalar2=None, op0=ALU.mult),
                          reads=[K_("kt"), K_("Ep")], writes=[K_("kh")])
                    if emit:
                        kb.op("dve", lambda: nc.vector.scalar_tensor_tensor(out=d["t1"][:, :], in0=rh, scalar=colv(HV_RK), in1=d["kmod"][:, :],
                                                                           op0=ALU.mult, op1=ALU.mult), reads=[K_("pm3"), K_("kmod"), "hv"], writes=[K_("t1")])
                        pbn, pbnk = bank()
                        kb.op("pe", lambda: nc.tensor.matmul(pbn[0:64, 0:CB], lhsT=o64[:, :], rhs=d["t1"][:, :], start=True, stop=True),
                              reads=["o64", K_("t1")], writes=[pbnk])
                        kb.op("dve", lambda: nc.vector.tensor_tensor(out=d["bon"][:, :], in0=pbn[0:64, 0:CB], in1=vh, op=ALU.mult),
                              reads=[pbnk, K_("pm3")], writes=[K_("bon")])
                    yield
                    ptm, ptmk = bank()
                    for j, (srcnm, sap) in enumerate((("bh", d["bh"][:, :]), ("kh", d["kh"][:, :]), ("pm3", vh), ("at", d["at"][:, :]))):
                        kb.op("pe", lambda j=j, sap=sap: nc.tensor.matmul(ptm[:, j * 64:(j + 1) * 64], lhsT=sap, rhs=ident[0:64, 0:64], start=True, stop=True),
                              reads=[K_(srcnm), "ident"], writes=[ptmk])
                    kb.op("act", lambda: nc.scalar.copy(out=d["TM"][:, :], in_=ptm[:, 0:192]), reads=[ptmk], writes=[K_("TM")])
                    kb.op("act", lambda: nc.scalar.copy(out=d["X"][:, 0:64], in_=ptm[:, 192:256]), reads=[ptmk], writes=[K_("X")])
                    bhT, khT, VT = d["TM"][:, 0:64], d["TM"][:, 64:128], d["TM"][:, 128:192]
                    yield
                    psa, psak = bank()
                    psb, psbk = bank()
                    for j, (l_, r_) in enumerate((("bt", "at"), ("at", "bt"), ("kt", "at"), ("bt", "rt"))):
                        kb.op("pe", lambda j=j, l_=l_, r_=r_: nc.tensor.matmul(psa[:, j * CB:(j + 1) * CB], lhsT=d[l_][:, :], rhs=d[r_][:, :], start=True, stop=True),
                              reads=[K_(l_), K_(r_)], writes=[psak])
                    kb.op("pe", lambda: nc.tensor.matmul(psb[:, 0:CB], lhsT=d["kt"][:, :], rhs=d["rt"][:, :], start=True, stop=True),
                          reads=[K_("kt"), K_("rt")], writes=[psbk])
                    kb.op("dve", lambda: nc.vector.tensor_tensor(out=d["SA"][:, :], in0=psa[:, :], in1=mkA[:, z, :], op=ALU.mult),
                          reads=[psak, "mkA"], writes=[K_("SA")])
                    kb.op("dve", lambda: nc.vector.tensor_tensor(out=d["SB"][:, :], in0=psb[:, 0:CB], in1=mkB[:, z, :], op=ALU.mult),
                          reads=[psbk, "mkB"], writes=[K_("SB")])
                    LakT, MrbT, MrkT = d["SA"][:, 2 * CB:3 * CB], d["SA"][:, 3 * CB:4 * CB], d["SB"][:, :]
                    yield
                    plv, plvk = bank()
                    kb.op("pe", lambda: nc.tensor.matmul(plv[:, 0:64], lhsT=LakT, rhs=VT, start=True, stop=True),
                          reads=[K_("SA"), K_("TM")], writes=[plvk])
                    kb.op("act", lambda: nc.scalar.copy(out=d["X"][:, 64:128], in_=plv[:, 0:64]), reads=[plvk], writes=[K_("X")])
                    yield
                    LabT_, Lab_ = d["SA"][:, 0:CB], d["SA"][:, CB:2 * CB]
                    Tm, TTm, P1s, tq = d["Tm"], d["TT"], d["P1s"], d["tq"]
                    kb.op("dve", lambda: nc.vector.tensor_tensor(out=Tm[:, :], in0=Lab_, in1=mq[:, z, 0, :], op=ALU.mult),
                          reads=[K_("SA"), "mq"], writes=[K_("Tm")])
                    kb.op("dve", lambda: nc.vector.tensor_tensor(out=Tm[:, :], in0=Tm[:, :], in1=ident[:, :], op=ALU.add),
                          reads=[K_("Tm"), "ident"], writes=[K_("Tm")])
                    kb.op("dve", lambda: nc.vector.tensor_tensor(out=TTm[:, :], in0=LabT_, in1=mq[:, z, 7, :], op=ALU.mult),
                          reads=[K_("SA"), "mq"], writes=[K_("TT")])
                    kb.op("dve", lambda: nc.vector.tensor_tensor(out=TTm[:, :], in0=TTm[:, :], in1=ident[:, :], op=ALU.add),
                          reads=[K_("TT"), "ident"], writes=[K_("TT")])
                    for lv in range(1, 7):
                        p1, p1k = bank()
                        kb.op("pe", lambda: nc.tensor.matmul(p1[:, 0:CB], lhsT=LabT_, rhs=Tm[:, :], start=True, stop=True),
                              reads=[K_("SA"), K_("Tm")], writes=[p1k])
                        kb.op("act", lambda: nc.scalar.copy(out=P1s[:, :], in_=p1[:, 0:CB]), reads=[p1k], writes=[K_("P1s")])
                        yield
                        p2, p2k = bank()
                        kb.op("pe", lambda: nc.tensor.matmul(p2[:, 0:CB], lhsT=TTm[:, :], rhs=P1s[:, :], start=True, stop=True),
                              reads=[K_("TT"), K_("P1s")], writes=[p2k])
                        kb.op("dve", lambda lv=lv: nc.vector.tensor_tensor(out=tq[:, :], in0=p2[:, 0:CB], in1=mq[:, z, lv, :], op=ALU.mult),
                              reads=[p2k, "mq"], writes=[K_("tq")])
                        kb.op("dve", lambda: nc.vector.tensor_tensor(out=Tm[:, :], in0=Tm[:, :], in1=tq[:, :], op=ALU.add),
                              reads=[K_("Tm"), K_("tq")], writes=[K_("Tm")])
                        yield
                        p3, p3k = bank()
                        kb.op("pe", lambda: nc.tensor.matmul(p3[:, 0:CB], lhsT=tq[:, :], rhs=ident[:, :], start=True, stop=True),
                              reads=[K_("tq"), "ident"], writes=[p3k])
                        kb.op("dve", lambda: nc.vector.tensor_tensor(out=TTm[:, :], in0=TTm[:, :], in1=p3[:, 0:CB], op=ALU.add),
                              reads=[K_("TT"), p3k], writes=[K_("TT")])
                        yield
                    yield
                    pw, pwk = bank()
                    kb.op("pe", lambda: nc.tensor.matmul(pw[0:64, 0:CB], lhsT=d["X"][:, 0:64], rhs=TTm[:, :], start=True, stop=True),
                          reads=[K_("X"), K_("TT")], writes=[pwk])
                    kb.op("act", lambda: nc.scalar.copy(out=d["WT"][:, :], in_=pw[0:64, 0:CB]), reads=[pwk], writes=[K_("WT")])
                    pu0, pu0k = bank()
                    kb.op("pe", lambda: nc.tensor.matmul(pu0[:, 0:64], lhsT=TTm[:, :], rhs=d["X"][:, 64:128], start=True, stop=True),
                          reads=[K_("TT"), K_("X")], writes=[pu0k])
                    kb.op("act", lambda: nc.scalar.copy(out=d["U0"][:, :], in_=pu0[:, 0:64]), reads=[pu0k], writes=[K_("U0")])
                    yield
                    Hh = Hs[:, h, :]
                    Hk = ("H", h)
                    pu_, puk_ = bank()
                    kb.op("pe", lambda: nc.tensor.matmul(pu_[:, 0:64], lhsT=d["WT"][:, :], rhs=Hh, start=True, stop=True),
                          reads=[K_("WT"), Hk], writes=[puk_])
                    kb.op("dve", lambda: nc.vector.tensor_tensor(out=d["U"][:, :], in0=d["U0"][:, :], in1=pu_[:, 0:64], op=ALU.add),
                          reads=[K_("U0"), puk_], writes=[K_("U")])
                    yield
                    if emit:
                        py, pyk = bank()
                        kb.op("pe", lambda: nc.tensor.matmul(py[0:64, 0:CB], lhsT=Hh, rhs=d["rt"][:, :], start=True, stop=True),
                              reads=[Hk, K_("rt")], writes=[pyk])
                        kb.op("pe", lambda: nc.tensor.matmul(py[0:64, CB:2 * CB], lhsT=d["U"][:, :], rhs=MrbT, start=True, stop=False),
                              reads=[K_("U"), K_("SA")], writes=[pyk])
                        kb.op("pe", lambda: nc.tensor.matmul(py[0:64, CB:2 * CB], lhsT=VT, rhs=MrkT, start=False, stop=True),
                              reads=[K_("TM"), K_("SB")], writes=[pyk])
                    yield
                    ph, phk = bank()
                    kb.op("pe", lambda: nc.tensor.matmul(ph[0:64, 0:64], lhsT=bhT, rhs=d["U"][:, :], start=True, stop=False),
                          reads=[K_("TM"), K_("U")], writes=[phk])
                    kb.op("pe", lambda: nc.tensor.matmul(ph[0:64, 0:64], lhsT=khT, rhs=VT, start=False, stop=True),
                          reads=[K_("TM")], writes=[phk])
                    kb.op("dve", lambda: nc.vector.scalar_tensor_tensor(out=Hh, in0=Hh, scalar=PC, in1=ph[0:64, 0:64], op0=ALU.mult, op1=ALU.add),
                          reads=[Hk, K_("Ep"), phk], writes=[Hk])
                    if cfg.get("debug") and z == 0 and (not emit) and c0 == COL_CTX and h < 2:
                        def dump(slot, ap, key, P, F_):
                            kb.dma("sp", f"dbg{slot}", dbgT[h, slot][0:P, 0:F_], ap, reads=[key])
                        dump(0, d["Ep"][:, :], K_("Ep"), 64, CB)
                        dump(1, d["at"][:, :], K_("at"), 64, CB)
                        dump(2, d["bt"][:, :], K_("bt"), 64, CB)
                        dump(3, d["kt"][:, :], K_("kt"), 64, CB)
                        dump(4, d["rt"][:, :], K_("rt"), 64, CB)
                        dump(5, d["SA"][:, :], K_("SA"), 128, 512)
                        dump(6, d["Tm"][:, :], K_("Tm"), 128, 128)
                        dump(7, d["U"][:, :], K_("U"), 128, 64)
                        dump(8, Hh, Hk, 64, 64)
                        dump(9, d["TM"][:, :], K_("TM"), 128, 192)
                        dump(10, d["SB"][:, :], K_("SB"), 128, 128)
                        dump(11, sgw[:, h * 64:(h + 1) * 64], "sgw", 128, 64)
                    if not emit:
                        return
                    if z == 0:
                        kb.op("act", lambda: nc.scalar.copy(out=d["yv"][:, :], in_=py[0:64, 0:CB]), reads=[pyk], writes=[K_("yv")])
                        kb.op("dve", lambda: nc.vector.tensor_tensor(out=d["yv"][:, :], in0=d["yv"][:, :], in1=py[0:64, CB:2 * CB], op=ALU.add),
                              reads=[K_("yv"), pyk], writes=[K_("yv")])
                        kb.dma("sp", f"sty{i}", yS[h][:, tok0:tok0 + CB], d["yv"][:, :], reads=[K_("yv")], writes=["hbm_y"])
                        kb.dma("sp", f"stb{i}", bS[h][:, tok0:tok0 + CB], d["bon"][:, :], reads=[K_("bon")], writes=["hbm_b"])
                        return
                    yield
                    kb.dma("sp", f"ldy{i}", d["y0"][:, :], yS[h][:, tok0:tok0 + CB], reads=["hbm_y"], writes=[K_("y0")])
                    kb.dma("sp", f"ldb{i}", d["b0"][:, :], bS[h][:, tok0:tok0 + CB], reads=["hbm_b"], writes=[K_("b0")])
                    kb.op("dve", lambda: nc.vector.tensor_tensor(out=d["yv"][:, :], in0=d["y0"][:, :], in1=py[0:64, 0:CB], op=ALU.add),
                          reads=[K_("y0"), pyk], writes=[K_("yv")])
                    kb.op("dve", lambda: nc.vector.tensor_tensor(out=d["yv"][:, :], in0=d["yv"][:, :], in1=py[0:64, CB:2 * CB], op=ALU.add),
                          reads=[K_("yv"), pyk], writes=[K_("yv")])
                    pmn, pmnk = bank()
                    kb.op("pe", lambda: nc.tensor.matmul(pmn[0:64, 0:CB], lhsT=o64m[:, :], rhs=d["yv"][:, :], start=True, stop=True),
                          reads=["o64m", K_("yv")], writes=[pmnk])
                    kb.op("dve", lambda: nc.vector.tensor_tensor(out=d["yc"][:, :], in0=d["yv"][:, :], in1=pmn[0:64, 0:CB], op=ALU.subtract),
                          reads=[K_("yv"), pmnk], writes=[K_("yc")])
                    kb.op("dve", lambda: nc.vector.tensor_tensor(out=d["t1"][:, :], in0=d["yc"][:, :], in1=d["yc"][:, :], op=ALU.mult),
                          reads=[K_("yc")], writes=[K_("t1")])
                    yield
                    pvr, pvrk = bank()
                    kb.op("pe", lambda: nc.tensor.matmul(pvr[0:64, 0:CB], lhsT=o64m[:, :], rhs=d["t1"][:, :], start=True, stop=True),
                          reads=["o64m", K_("t1")], writes=[pvrk])
                    kb.op("act", lambda: nc.scalar.activation(out=d["t2"][:, :], in_=pvr[0:64, 0:CB], func=AF.Sqrt, bias=gne[:], scale=1.0),
                          reads=[pvrk, "gne"], writes=[K_("t2")])
                    kb.op("dve", lambda: nc.vector.reciprocal(out=d["t2"][:, :], in_=d["t2"][:, :]), reads=[K_("t2")], writes=[K_("t2")])
                    kb.op("dve", lambda: nc.vector.tensor_tensor(out=d["yc"][:, :], in0=d["yc"][:, :], in1=d["t2"][:, :], op=ALU.mult),
                          reads=[K_("yc"), K_("t2")], writes=[K_("yc")])
                    kb.op("dve", lambda: nc.vector.tensor_scalar(out=d["yc"][:, :], in0=d["yc"][:, :], scalar1=colv(HV_LW), scalar2=colv(HV_LB),
                                                                op0=ALU.mult, op1=ALU.add), reads=[K_("yc"), "hv"], writes=[K_("yc")])
                    kb.op("dve", lambda: nc.vector.tensor_tensor(out=d["b0"][:, :], in0=d["b0"][:, :], in1=d["bon"][:, :], op=ALU.add),
                          reads=[K_("b0"), K_("bon")], writes=[K_("b0")])
                    kb.op("dve", lambda: nc.vector.tensor_tensor(out=d["yc"][:, :], in0=d["yc"][:, :], in1=d["b0"][:, :], op=ALU.add),
                          reads=[K_("yc"), K_("b0")], writes=[K_("yc")])
                    yield
                    pg_, pgk_ = bank()
                    kb.op("pe", lambda: nc.tensor.matmul(pg_[0:64, 0:CB], lhsT=g2t[:, h * 64:(h + 1) * 64], rhs=sgl[:, :], start=True, stop=True),
                          reads=["g2t", "sgl"], writes=[pgk_])
                    kb.op("dve", lambda: nc.vector.tensor_tensor(out=d["ob"][:, :], in0=d["yc"][:, :], in1=pg_[0:64, 0:CB], op=ALU.mult),
                          reads=[K_("yc"), pgk_], writes=[K_("ob")])
                    kb.dma("sp", f"strw{i}", rwS_h[h][:, tok0:tok0 + CB], d["ob"][:, :], reads=[K_("ob")], writes=["hbm_rw"])

                for z in range(2):
                    kb.op("dve", lambda: nc.vector.memset(Hs[:, :, :], 0.0), reads=[("H", h) for h in range(16)],
                          writes=[("H", h) for h in range(16)])
                    for kind, L, cbase, emit in (("ctx", CTX, COL_CTX, False), ("lat", T, COL_LAT, True)):
                        nb = L // CB
                        blocks = list(range(nb)) if z == 0 else list(range(nb - 1, -1, -1))
                        for blk in blocks:
                            c0 = cbase + blk * CB
                            tok0 = blk * CB
                            kb.dma("sp", "ldshr", shr[:, :, :], prS_v[:, 24:27, c0 - 1:c0 + CB + 1], reads=["hbm_pr"], writes=["shr"])
                            kb.op("dve", lambda: nc.vector.tensor_tensor(out=shs[:, :, :], in0=shr[:, :, 0:CB], in1=shr[:, :, 2:CB + 2], op=ALU.add),
                                  reads=["shr"], writes=["shs"])
                            for g in range(3):
                                kb.op("dve", lambda g=g: nc.vector.tensor_scalar(out=shq[:, g, :], in0=shr[:, g, 1:CB + 1], scalar1=sv1[:, g:g + 1],
                                                                                scalar2=None, op0=ALU.mult), reads=["shr", "sv1"], writes=["shq"])
                                kb.op("dve", lambda g=g: nc.vector.scalar_tensor_tensor(out=psh[:, g, :], in0=shs[:, g, :], scalar=svh[:, g:g + 1],
                                                                                       in1=shq[:, g, :], op0=ALU.mult, op1=ALU.add),
                                      reads=["shs", "shq", "svh"], writes=["psh"])
                            kb.op("act", lambda: nc.scalar.activation(out=tlw[:, :], in_=psh[:, 0, :], func=AF.Tanh), reads=["psh"], writes=["tlw"])
                            for half in range(2):
                                pwr, pwrk = bank()
                                kb.op("pe", lambda half=half: nc.tensor.matmul(pwr[:, :], lhsT=tlw[:, :], rhs=w2p[:, z, half * 512:(half + 1) * 512],
                                                                              start=True, stop=False), reads=["tlw", "w2p"], writes=[pwrk])
                                kb.op("pe", lambda half=half: nc.tensor.matmul(pwr[:, :], lhsT=onesr[0:1, :], rhs=w0r[0:1, z, half * 512:(half + 1) * 512],
                                                                              start=False, stop=True), reads=["onesr", "w0r"], writes=[pwrk])
                                kb.op("act", lambda half=half: nc.scalar.activation(out=sgw[:, half * 512:(half + 1) * 512], in_=pwr[:, :], func=AF.Sigmoid),
                                      reads=[pwrk], writes=["sgw"])
                            if emit and z == 1:
                                kb.op("act", lambda: nc.scalar.activation(out=sgl[:, :], in_=psh[:, 2, :], func=AF.Sigmoid), reads=["psh"], writes=["sgl"])
                            for h0 in range(0, 16, NSET):
                                gens = [head_block(z, h0 + i_, c0, tok0, emit, i_) for i_ in range(NSET)]
                                while gens:
                                    for g_ in list(gens):
                                        try:
                                            next(g_)
                                        except StopIteration:
                                            gens.remove(g_)
                barrier()
                NROT[0] = 6

        with contextlib.ExitStack() as p2:
            alloc_tile_bufs(p2)
            x, sq, hb, act, rstd = Bf["xt0"], Bf["sq"], Bf["hb"], Bf["act"], Bf["rstd"]
            kT_sb = sbs(p2, "kT_sb", [128, NK], BF16)
            v_sb = sbs(p2, "v_sb", [128, NKC, 128], BF16)
            q_sb = [sbs(p2, f"q_sb{i}", [128, TT], BF16) for i in range(2)]
            pT = [sbs(p2, f"pT{i}", [128, TT], BF16) for i in range(3)]
            if rw_on:
                rw_sb = sbs(p2, "rw_sb", [128, 8, TT], BF16)
            po, pok = banks[6], ("bank", 6)
            pd, pdk = banks[7], ("bank", 7)
            at = sq
            n = TT
            for it in range(NT):
                t0 = it * TT
                cnt = 0
                for g in range(NKV):
                    kb.dma("sp", "ldk", kT_sb[:, :], kS[g], reads=["hbm_k"], writes=["kT_sb"])
                    kb.dma("sp", "ldv", v_sb[:, :, :], vS[g], reads=["hbm_v"], writes=["v_sb"])
                    for hh in range(4):
                        h = g * 4 + hh
                        qb = q_sb[h % 2]
                        qk_ = ("q_sb", h % 2)
                        kb.dma("sp", f"ldq{h % 2}", qb[:, 0:n], qS[h][:, t0:t0 + n], reads=["hbm_q"], writes=[qk_])
                        for kc in range(NKC):
                            pb, pk = bank()
                            kb.op("pe", lambda kc=kc: nc.tensor.matmul(pb[:, 0:n], lhsT=kT_sb[:, kc * 128:(kc + 1) * 128],
                                                                      rhs=qb[:, 0:n], start=True, stop=True),
                                  reads=["kT_sb", qk_], writes=[pk])
                            p_ = pT[cnt % 3]
                            p_k = ("pT", cnt % 3)
                            cnt += 1
                            kb.op("act", lambda p_=p_: nc.scalar.activation(out=p_[:, 0:n], in_=pb[:, 0:n], func=AF.Exp, scale=ATT_SCALE),
                                  reads=[pk], writes=[p_k])
                            kb.op("pe", lambda kc=kc, p_=p_: nc.tensor.matmul(po[:, 0:n], lhsT=v_sb[:, kc, :], rhs=p_[:, 0:n],
                                                                             start=(kc == 0), stop=(kc == NKC - 1)),
                                  reads=["v_sb", p_k], writes=[pok])
                            kb.op("pe", lambda kc=kc, p_=p_: nc.tensor.matmul(pd[:, 0:n], lhsT=ones_1[:], rhs=p_[:, 0:n],
                                                                             start=(kc == 0), stop=(kc == NKC - 1)),
                                  reads=["ones", p_k], writes=[pdk])
                        kb.op("dve", lambda: nc.vector.reciprocal(out=rstd[:, 0:n], in_=pd[:, 0:n]), reads=[pdk], writes=["rstd"])
                        kb.op("dve", lambda h=h: nc.vector.tensor_tensor(out=at[:, h, 0:n], in0=po[:, 0:n], in1=rstd[:, 0:n], op=ALU.mult),
                              reads=[pok, "rstd"], writes=["sq"])
                kb.dma("sp", "ldga", act[:, 0:16, 0:n], gaS_v[:, :, t0:t0 + n], reads=["hbm_ga"],
                       writes=[("act", j) for j in range(16)])
                kb.dma("sp", "ldx", x[:, :, 0:n], x1T_v[:, :, t0:t0 + n], reads=["hbm_x1"], writes=[xk])
                if rw_on:
                    kb.dma("sp", "ldgr", hb[:, :, 0:n], grS_v[:, :, t0:t0 + n], reads=["hbm_gr"], writes=["hb"])
                    kb.dma("sp", "ldrw", rw_sb[:, :, 0:n], rwS_v[:, :, t0:t0 + n], reads=["hbm_rw"], writes=["rw_sb"])

                def oa_epi(oc, pb, pk):
                    kb.op("dve", lambda: nc.vector.tensor_tensor(out=act[:, 16 + oc, 0:n], in0=pb[:, 0:n], in1=act[:, oc, 0:n], op=ALU.mult),
                          reads=[pk, ("act", oc)], writes=[("act", 16 + oc)])
                linear(at, lambda kc: "sq", n, 16, oa_epi)
                if rw_on:
                    def or_epi(oc, pb, pk):
                        tb, tk = tmpbuf()
                        kb.op("dve", lambda: nc.vector.tensor_tensor(out=tb[:, 0:n], in0=pb[:, 0:n], in1=hb[:, oc, 0:n], op=ALU.mult),
                              reads=[pk, "hb"], writes=[tk])
                        kb.op("dve", lambda: nc.vector.tensor_tensor(out=act[:, 16 + oc, 0:n], in0=act[:, 16 + oc, 0:n], in1=tb[:, 0:n], op=ALU.add),
                              reads=[tk, ("act", 16 + oc)], writes=[("act", 16 + oc)])
                    linear(rw_sb, lambda kc: "rw_sb", n, 16, or_epi, kcn=8)
                m_v = act[:, 16:32, :]

                def out_epi(oc, pb, pk):
                    kb.op("dve", lambda: nc.vector.scalar_tensor_tensor(
                        out=x[:, oc, 0:n], in0=pb[:, 0:n], scalar=mG[:, 1, oc, 0:1], in1=x[:, oc, 0:n],
                        op0=ALU.mult, op1=ALU.add), reads=[pk, "mG", xk], writes=[xk])
                linear(m_v, lambda kc: ("act", 16 + kc), n, 16, out_epi)
                ffn(x, xk, 2, 0, n)
                final_norm_store(x, xk, t0, n)
            kb._wait("sp", ("st", kb.cnt["st"]))
        assert ws.used == len(plan), (ws.used, len(plan))
        print(f"[build] instructions={kb.n_inst} waits={kb.n_wait}")
    return nc


def host_prep(inputs, b):
    x = np.asarray(inputs["x"], np.float32)
    v = np.zeros((128, NV), np.float32)
    cl = fm_vec(inputs["c"][b])
    cc = fm_vec(inputs["c_ctx"])
    v[:, V_C:V_C + 32:2] = cl
    v[:, V_C + 1:V_C + 32:2] = cc
    v[:, V_BMOD:V_BMOD + 144] = fm_vec(inputs["b_mod"][0])
    v[:, V_G1:V_G1 + 16] = fm_vec(inputs["norm_ffn1"][0])
    v[:, V_GMIX:V_GMIX + 16] = fm_vec(inputs["norm_mix"][0])
    v[:, V_G2:V_G2 + 16] = fm_vec(inputs["norm_ffn2"][0])
    v[:, V_GF:V_GF + 16] = fm_vec(inputs["norm_final"])
    v[:, V_QN] = inputs["q_norm"][0]
    v[:, V_KN] = inputs["k_norm"][0]
    return {"xT": np.ascontiguousarray(x[b].T), "cxT": np.ascontiguousarray(np.asarray(inputs["ctx"], np.float32)[b].T), "vecs": v}


def host_prep_rwkv(inputs):
    f = lambda k: np.asarray(inputs[k], np.float32)[0]
    w_in = f("w_in")
    o_rw = 3072
    o_gr = 3072 + 3456 + 2048
    mu = f("rwkv_mu")
    hv = np.zeros((64, NHV), np.float32)
    for g in range(3):
        hv[:, HV_MU + g * 16:HV_MU + (g + 1) * 16] = head_fm(mu[g * 1024:(g + 1) * 1024])
    hv[:, HV_KK:HV_KK + 16] = head_fm(f("k_k"))
    hv[:, HV_KA:HV_KA + 16] = head_fm(f("k_a"))
    a0 = f("a0")
    hv[:, HV_A0:HV_A0 + 16] = head_fm(a0[0])
    hv[:, HV_A0 + 16:HV_A0 + 32] = head_fm(a0[1])
    hv[:, HV_RK:HV_RK + 16] = head_fm(f("r_k").reshape(-1))
    hv[:, HV_LW:HV_LW + 16] = head_fm(f("ln_x_w"))
    hv[:, HV_LB:HV_LB + 16] = head_fm(f("ln_x_b"))
    sv = np.ascontiguousarray(mu[3072:3456].reshape(3, 128).T)
    w2 = f("w2")
    a2 = f("a2")
    w2p = np.zeros((128, 2, 1024), np.float32)
    a2p = np.zeros((128, 2, 1024), np.float32)
    for z in range(2):
        w2p[z * 64:(z + 1) * 64, z, :] = w2[z]
        a2p[z * 64:(z + 1) * 64, z, :] = a2[z]
    tri, mkA, mkB, ident, mq = rwkv_consts()
    return {
        "wrw": prep_w(w_in[:, o_rw:o_rw + 3456]),
        "wgr": prep_w(w_in[:, o_gr:o_gr + 2048]),
        "wor": prep_w(f("w_or")),
        "w2p": w2p, "a2p": a2p, "g2": np.ascontiguousarray(f("g2")),
        "w0r": np.ascontiguousarray(f("w0").reshape(1, 2, 1024)),
        "hvec": hv, "svec": sv, "tri": tri, "mkA": mkA, "mkB": mkB, "ident": ident, "mq": mq,
    }


def kernel(**inputs):
    inputs = {k: np.asarray(v) for k, v in inputs.items()}
    stages = str(inputs.pop("_stages", "full"))
    ncores = int(inputs.pop("_ncores", 8))
    B, T, _ = inputs["x"].shape
    debug = bool(inputs.pop("_debug", False))
    cfg = {"T": T, "stages": stages, "debug": debug}
    shared = {
        "wmod": prep_w(inputs["w_mod"][0]),
        "w1i": prep_w(inputs["ffn1_w_in"][0]),
        "w1o": prep_w(inputs["ffn1_w_out"][0]),
    }
    if stages != "ffn1":
        w_in = inputs["w_in"][0]
        o_q, o_k, o_v, o_rw, o_ga = 0, 2048, 2560, 3072, 3072 + 3456
        cosT, sinT, perm = rope_tables(T)
        shared.update({
            "w2i": prep_w(inputs["ffn2_w_in"][0]),
            "w2o": prep_w(inputs["ffn2_w_out"][0]),
            "wq": prep_w(w_in[:, o_q:o_k]),
            "wk": prep_w(w_in[:, o_k:o_v]),
            "wv": prep_w(w_in[:, o_v:o_rw]),
            "wga": prep_w(w_in[:, o_ga:o_ga + 2048]),
            "woa": prep_w(inputs["w_oa"][0]),
            "wout": prep_w(inputs["w_out"][0]),
            "cosT": cosT, "sinT": sinT, "perm": perm,
        })
    if stages == "full":
        shared.update(host_prep_rwkv(inputs))
    nc = build_program(cfg)
    in_maps = []
    for core in range(ncores):
        b = core % B
        m = dict(shared)
        m.update(host_prep(inputs, b))
        in_maps.append(m)
    trace = bool(inputs.pop("_trace", False)) if "_trace" in inputs else False
    res = run_bass_kernel_spmd(nc, in_maps, core_ids=list(range(ncores)), **({"trace": True} if trace else {}))
    if trace:
        print("[trace] exec_time_ns", res.exec_time_ns)
    if debug:
        return res.results[0]
    out = np.stack([np.ascontiguousarray(res.results[b % ncores]["outT"].T) for b in range(B)], axis=0)
    return out.astype(np.float32)
```
